# Optimizing a Trainium2 kernel written in Bass

```python
import jax, jax.numpy as jnp
from jax import lax
import numpy as np

D_MODEL = 2048
BATCH = 4
SEQ = 2048
DEPTH = 1
DEC_BATCH = 128
DEC_SEQ = 1
PAST_LEN = 16384
PAGE_SIZE = 128

A_HEADS = 4
A_WIDTH = D_MODEL // 2
A_DK = A_WIDTH // A_HEADS
A_DV = A_WIDTH // A_HEADS
CONV_W = 4
B_HEADS = 4
B_WIDTH = D_MODEL // 2
B_KWIDTH = B_WIDTH // 2
B_DK = B_KWIDTH // B_HEADS
B_DV = B_WIDTH // B_HEADS
GATE_RANK = 16
GATE_TAU = 16.0
CHUNK = 64
ALPHA = (2 * DEPTH) ** 0.25
BETA = (8 * DEPTH) ** -0.25
LN_EPS = 1e-5

SEG_SIZES = (2 * A_WIDTH, A_WIDTH, A_HEADS, A_HEADS, A_WIDTH, A_WIDTH,
             B_KWIDTH, B_KWIDTH, B_WIDTH, GATE_RANK, B_WIDTH, D_MODEL, D_MODEL)
N_IN = (2 * A_WIDTH + A_WIDTH + 2 * A_HEADS + 2 * A_WIDTH
        + 2 * B_KWIDTH + B_WIDTH + GATE_RANK + B_WIDTH + 2 * D_MODEL)

kernel_name = "mlstm_gla_gated_hybrid_step"


def _split_cols(p):
    idx = []
    acc = 0
    for s in SEG_SIZES[:-1]:
        acc += s
        idx.append(acc)
    return jnp.split(p, idx, axis=-1)


def _heads(a, h):
    bt, l, w = a.shape
    return a.reshape(bt, l, h, w // h).transpose(0, 2, 1, 3)


def _to_chunks(a, ch):
    bt, h, l = a.shape[:3]
    a = a.reshape((bt, h, l // ch, ch) + a.shape[3:])
    return jnp.moveaxis(a, 2, 0)


def _from_chunks(a):
    a = jnp.moveaxis(a, 0, 2)
    return a.reshape(a.shape[:2] + (a.shape[2] * a.shape[3],) + a.shape[4:])


def _layernorm(x, g, b):
    xf = x.astype(jnp.float32)
    mu = jnp.mean(xf, axis=-1, keepdims=True)
    var = jnp.mean(jnp.square(xf - mu), axis=-1, keepdims=True)
    return (xf - mu) * lax.rsqrt(var + LN_EPS) * g.astype(jnp.float32) + b.astype(jnp.float32)


def _mlstm_chunk(q, k, v, itil, logf, C, n, m):
    L = q.shape[2]
    causal = jnp.tril(jnp.ones((L, L), dtype=bool))
    b = jnp.cumsum(logf, axis=-1)
    dmat = b[..., :, None] - b[..., None, :] + itil[..., None, :]
    dmat = jnp.where(causal, dmat, -jnp.inf)
    inter = b + m[..., None]
    m_t = jnp.maximum(inter, jnp.max(dmat, axis=-1))
    w = jnp.exp(dmat - m_t[..., None])
    decay = jnp.exp(inter - m_t)
    s = jnp.einsum('bhtd,bhsd->bhts', q, k) * w
    num = decay[..., None] * jnp.einsum('bhtd,bhde->bhte', q, C) + jnp.einsum('bhts,bhse->bhte', s, v)
    den = decay * jnp.einsum('bhtd,bhd->bht', q, n) + jnp.sum(s, axis=-1)
    h = num / jnp.maximum(jnp.abs(den), jnp.exp(-m_t))[..., None]
    m_new = m_t[..., -1]
    wk = jnp.exp(b[..., -1:] - b + itil - m_new[..., None])
    dec = jnp.exp(b[..., -1] + m - m_new)
    C_new = dec[..., None, None] * C + jnp.einsum('bhs,bhsd,bhse->bhde', wk, k, v)
    n_new = dec[..., None] * n + jnp.einsum('bhs,bhsd->bhd', wk, k)
    return h, C_new, n_new, m_new


def _gla_chunk(q, k, v, loga, S):
    L = q.shape[2]
    causal = jnp.tril(jnp.ones((L, L), dtype=bool))[:, :, None]
    bc = jnp.cumsum(loga, axis=2)
    diff = bc[:, :, :, None, :] - bc[:, :, None, :, :]
    diff = jnp.where(causal, diff, -jnp.inf)
    a = jnp.einsum('bhtd,bhsd,bhtsd->bhts', q, k, jnp.exp(diff))
    o = jnp.einsum('bhtd,bhde->bhte', q * jnp.exp(bc), S) + jnp.einsum('bhts,bhse->bhte', a, v)
    last = bc[:, :, -1:, :]
    S_new = jnp.exp(last[:, :, 0, :])[..., None] * S + jnp.einsum('bhsd,bhse->bhde', k * jnp.exp(last - bc), v)
    return o, S_new


def _mlstm(q, k, v, itil, logf, C, n, m):
    L = q.shape[2]
    ch = CHUNK if L % CHUNK == 0 else L
    xs = tuple(_to_chunks(a, ch) for a in (q, k, v, itil, logf))

    def step(carry, inp):
        h, C2, n2, m2 = _mlstm_chunk(*inp, *carry)
        return (C2, n2, m2), h

    (C, n, m), h = lax.scan(step, (C, n, m), xs)
    return _from_chunks(h), C, n, m


def _gla(q, k, v, loga, S):
    L = q.shape[2]
    ch = CHUNK if L % CHUNK == 0 else L
    xs = tuple(_to_chunks(a, ch) for a in (q, k, v, loga))

    def step(S_c, inp):
        o, S2 = _gla_chunk(*inp, S_c)
        return S2, o

    S, o = lax.scan(step, S, xs)
    return _from_chunks(o), S


def _layer(x, conv_buf, C, n, m, S, w_in, conv_w, conv_b, b_i, b_f, a_norm_g,
           w_gate_up, b_gate, b_norm_g, w_pa, w_pb, w_out, ln_g, ln_b):
    f32 = jnp.float32
    bt, L, _ = x.shape
    p = x @ w_in
    (qk_pre, a_v, a_i, a_f, a_o, a_z, b_q, b_k, b_v, b_g, b_z, g_a, g_b) = _split_cols(p)
    ext = jnp.concatenate([conv_buf.astype(qk_pre.dtype), qk_pre], axis=1)
    conv = conv_b
    for j in range(CONV_W):
        conv = conv + ext[:, j:j + L] * conv_w[j]
    new_buf = ext[:, L:]
    qk = jax.nn.silu(conv)
    a_q, a_k = jnp.split(qk, 2, axis=-1)
    q = _heads(a_q, A_HEADS).astype(f32)
    k = _heads(a_k, A_HEADS).astype(f32) * (A_DK ** -0.5)
    v = _heads(a_v, A_HEADS).astype(f32)
    itil = (a_i + b_i).astype(f32).transpose(0, 2, 1)
    logf = jax.nn.log_sigmoid((a_f + b_f).astype(f32)).transpose(0, 2, 1)
    h, C_new, n_new, m_new = _mlstm(q, k, v, itil, logf, C.astype(f32), n.astype(f32), m.astype(f32))
    h = h.transpose(0, 2, 1, 3)
    mu = jnp.mean(h, axis=-1, keepdims=True)
    var = jnp.mean(jnp.square(h - mu), axis=-1, keepdims=True)
    hn = (h - mu) * lax.rsqrt(var + LN_EPS) * a_norm_g.astype(f32).reshape(A_HEADS, A_DV)
    ya = hn.reshape(bt, L, A_WIDTH) * jax.nn.sigmoid(a_o.astype(f32)) * jax.nn.silu(a_z.astype(f32))
    gq = _heads(b_q, B_HEADS).astype(f32) * (B_DK ** -0.5)
    gk = _heads(b_k, B_HEADS).astype(f32)
    gv = _heads(b_v, B_HEADS).astype(f32)
    loga = jax.nn.log_sigmoid((b_g @ w_gate_up + b_gate).astype(f32)) / GATE_TAU
    loga = _heads(loga, B_HEADS)
    o, S_new = _gla(gq, gk, gv, loga, S.astype(f32))
    o = o.transpose(0, 2, 1, 3)
    on = o * lax.rsqrt(jnp.mean(jnp.square(o), axis=-1, keepdims=True) + LN_EPS)
    on = on * b_norm_g.astype(f32).reshape(B_HEADS, B_DV)
    yb = on.reshape(bt, L, B_WIDTH) * jax.nn.silu(b_z.astype(f32))
    ya = ya.astype(x.dtype) @ w_pa
    yb = yb.astype(x.dtype) @ w_pb
    merged = jax.nn.sigmoid(g_a) * ya + jax.nn.sigmoid(g_b) * yb
    y = merged @ w_out
    out = _layernorm(ALPHA * x + y, ln_g, ln_b).astype(x.dtype)
    dt = x.dtype
    return out, C_new.astype(dt), n_new.astype(dt), m_new.astype(dt), new_buf.astype(dt), S_new.astype(dt)


def setup_inputs(seed: int = 0) -> dict:
    key = jax.random.key(seed)
    ks = jax.random.split(key, 24)
    nrm = jax.random.normal
    f32 = jnp.float32
    return {
        "x_prompt": nrm(ks[0], (BATCH, SEQ, D_MODEL), f32),
        "x_sample": nrm(ks[1], (DEC_BATCH, DEC_SEQ, D_MODEL), f32),
        "state_mlstm_C": 0.1 * nrm(ks[2], (DEPTH, DEC_BATCH, A_HEADS, A_DK, A_DV), f32),
        "state_mlstm_n": 0.1 * nrm(ks[3], (DEPTH, DEC_BATCH, A_HEADS, A_DK), f32),
        "state_mlstm_m": nrm(ks[4], (DEPTH, DEC_BATCH, A_HEADS), f32),
        "state_conv": nrm(ks[5], (DEPTH, DEC_BATCH, CONV_W - 1, 2 * A_WIDTH), f32),
        "state_gla_S": 0.1 * nrm(ks[6], (DEPTH, DEC_BATCH, B_HEADS, B_DK, B_DV), f32),
        "w_in": nrm(ks[7], (DEPTH, D_MODEL, N_IN), f32) * D_MODEL ** -0.5,
        "conv_w": nrm(ks[8], (DEPTH, CONV_W, 2 * A_WIDTH), f32) * CONV_W ** -0.5,
        "conv_b": 0.01 * nrm(ks[9], (DEPTH, 2 * A_WIDTH), f32),
        "b_i": 0.1 * nrm(ks[10], (DEPTH, A_HEADS), f32),
        "b_f": 3.0 + 0.1 * nrm(ks[11], (DEPTH, A_HEADS), f32),
        "a_norm_g": 1.0 + 0.02 * nrm(ks[12], (DEPTH, A_WIDTH), f32),
        "w_gate_up": nrm(ks[13], (DEPTH, GATE_RANK, B_KWIDTH), f32) * GATE_RANK ** -0.5,
        "b_gate": 0.01 * nrm(ks[14], (DEPTH, B_KWIDTH), f32),
        "b_norm_g": 1.0 + 0.02 * nrm(ks[15], (DEPTH, B_WIDTH), f32),
        "w_pa": nrm(ks[16], (DEPTH, A_WIDTH, D_MODEL), f32) * (A_WIDTH ** -0.5 * BETA),
        "w_pb": nrm(ks[17], (DEPTH, B_WIDTH, D_MODEL), f32) * (B_WIDTH ** -0.5 * BETA),
        "w_out": nrm(ks[18], (DEPTH, D_MODEL, D_MODEL), f32) * (D_MODEL ** -0.5 * BETA),
        "ln_g": 1.0 + 0.02 * nrm(ks[19], (DEPTH, D_MODEL), f32),
        "ln_b": 0.01 * nrm(ks[20], (DEPTH, D_MODEL), f32),
    }


def reference(x_prompt, x_sample, state_mlstm_C, state_mlstm_n, state_mlstm_m, state_conv,
              state_gla_S, w_in, conv_w, conv_b, b_i, b_f, a_norm_g, w_gate_up, b_gate,
              b_norm_g, w_pa, w_pb, w_out, ln_g, ln_b):
    f32 = jnp.float32
    xp = x_prompt
    xs = x_sample
    pC, pn, pm, pconv, pS = [], [], [], [], []
    sC, sn, sm, sconv, sS = [], [], [], [], []
    for d in range(DEPTH):
        params = (w_in[d], conv_w[d], conv_b[d], b_i[d], b_f[d], a_norm_g[d], w_gate_up[d],
                  b_gate[d], b_norm_g[d], w_pa[d], w_pb[d], w_out[d], ln_g[d], ln_b[d])
        C0 = jnp.zeros((BATCH, A_HEADS, A_DK, A_DV), f32)
        n0 = jnp.zeros((BATCH, A_HEADS, A_DK), f32)
        m0 = jnp.zeros((BATCH, A_HEADS), f32)
        buf0 = jnp.zeros((BATCH, CONV_W - 1, 2 * A_WIDTH), xp.dtype)
        S0 = jnp.zeros((BATCH, B_HEADS, B_DK, B_DV), f32)
        xp, c1, c2, c3, c4, c5 = _layer(xp, buf0, C0, n0, m0, S0, *params)
        pC.append(c1); pn.append(c2); pm.append(c3); pconv.append(c4); pS.append(c5)
        xs, e1, e2, e3, e4, e5 = _layer(xs, state_conv[d], state_mlstm_C[d], state_mlstm_n[d],
                                        state_mlstm_m[d], state_gla_S[d], *params)
        sC.append(e1); sn.append(e2); sm.append(e3); sconv.append(e4); sS.append(e5)
    return (xp, xs,
            jnp.stack(pC), jnp.stack(pn), jnp.stack(pm), jnp.stack(pconv), jnp.stack(pS),
            jnp.stack(sC), jnp.stack(sn), jnp.stack(sm), jnp.stack(sconv), jnp.stack(sS))
```

```python
import numpy as np
import concourse.bass as bass
import concourse.mybir as mybir
from concourse.bass_utils import run_bass_kernel_spmd

F32 = mybir.dt.float32
BF16 = mybir.dt.bfloat16
AF = mybir.ActivationFunctionType
ALU = mybir.AluOpType
AX = mybir.AxisListType

D = 2048
NP = 1024
NS = 16
NM = NP + NS
NCH = NP // 128
N_IN = 12312
ALPHA = 2.0 ** 0.25
EPS = 1e-5
SAME_SYNC = ('act', 'dve', 'pool', 'sp')

C_Q, C_K, C_V, C_I, C_F, C_O, C_Z = 0, 1024, 2048, 3072, 3076, 3080, 4104
C_BQ, C_BK, C_BV, C_BG, C_BZ, C_GA, C_GB = 5128, 5640, 6152, 7176, 7192, 8216, 10264


class Trk:
    def __init__(self, nc):
        self.nc = nc
        self.eng = dict(pe=nc.tensor, act=nc.scalar, dve=nc.vector, pool=nc.gpsimd, sp=nc.sync)
        self.sem = {k: nc.alloc_semaphore("sem_" + k) for k in self.eng}
        self.cnt = {k: 0 for k in self.eng}
        self.seen = {k: {} for k in self.eng}
        self.bufs = {}
        self.streams = {}
        self.ps_next = 0

    def _wait(self, eng, tok):
        key, sem, val = tok
        if key == eng and eng not in SAME_SYNC:
            return
        if self.seen[eng].get(key, 0) >= val:
            return
        self.eng[eng].wait_ge(sem, val)
        self.seen[eng][key] = val

    def _deps(self, eng, r, w):
        for k in r:
            b = self.bufs.get(k)
            if b and b[0]:
                self._wait(eng, b[0])
        for k in w:
            b = self.bufs.get(k)
            if b:
                if b[0]:
                    self._wait(eng, b[0])
                for t in b[1].values():
                    self._wait(eng, t)

    def _record(self, tok, r, w):
        for k in r:
            b = self.bufs.setdefault(k, [None, {}])
            b[1][tok[0]] = tok
        for k in w:
            self.bufs[k] = [tok, {}]

    def op(self, eng, fn, r=(), w=()):
        self._deps(eng, r, w)
        ins = fn(self.eng[eng])
        self.cnt[eng] += 1
        ins.then_inc(self.sem[eng], 1)
        self._record((eng, self.sem[eng], self.cnt[eng]), r, w)
        return ins

    def dma(self, eng, stream, out, in_, r=(), w=()):
        if stream not in self.streams:
            self.streams[stream] = [self.nc.alloc_semaphore("dsem_" + stream), 0]
        st = self.streams[stream]
        self._deps(eng, r, w)
        ins = self.eng[eng].dma_start(out=out, in_=in_)
        st[1] += 16
        ins.then_inc(st[0], 16)
        self._record(("dma:" + stream, st[0], st[1]), r, w)

    def barrier(self):
        names = list(self.eng)
        for e in names:
            for o in names:
                if o != e and self.cnt[o] > 0:
                    self._wait(e, (o, self.sem[o], self.cnt[o]))
            for s, (sem, val) in self.streams.items():
                if val > 0:
                    self._wait(e, ("dma:" + s, sem, val))

    def finish(self):
        for s, (sem, val) in self.streams.items():
            if val > 0:
                self._wait("sp", ("dma:" + s, sem, val))
        for o in self.eng:
            if o != "sp" and self.cnt[o] > 0:
                self._wait("sp", (o, self.sem[o], self.cnt[o]))


def build_nc():
    nc = bass.Bass("TRN2", target_bir_lowering=False)

    def din(name, shape):
        return nc.dram_tensor(name, list(shape), F32, kind="ExternalInput").ap()

    def dout(name, shape):
        return nc.dram_tensor(name, list(shape), F32, kind="ExternalOutput").ap()

    xT_d = din("xT", [D, NM]); xTp_d = din("xTp", [D, NP]); xtok_d = din("xtok", [NM, D])
    flag_d = din("flag", [128, 1])
    w_in_d = din("w_in", [D, N_IN]); w_pa_d = din("w_pa", [1024, D]); w_pb_d = din("w_pb", [1024, D])
    w_out_d = din("w_out", [D, D]); w_gu_d = din("w_gu", [16, 512])
    cw_d = din("cw", [128, 16 * 4]); cb_d = din("cb", [128, 16]); bgate_d = din("bgate", [128, 4])
    bi_d = din("bi", [4, 1]); bf_d = din("bf", [4, 1]); bi_row_d = din("bi_row", [16, 4]); bf_row_d = din("bf_row", [16, 4])
    gna_d = din("gna", [128, 8]); gnb_d = din("gnb", [128, 8])
    lng_d = din("lng", [128, D]); lnb_d = din("lnb", [128, D])
    ident_d = din("ident", [128, 128]); maskA_d = din("maskA", [128, 128]); maskB_d = din("maskB", [128, 128])
    rmask_d = din("rmask", [128, NM]); diag_d = din("diag16", [16, 16]); diagbc_d = din("diagbc", [128, 256])
    caug_in_d = din("caug_in", [16, 4, 256, 257]); s_in_d = din("s_in", [16, 4, 128, 256])
    m_in_d = din("m_in", [16, 4]); conv_in_d = din("conv_in", [128, 16 * 3 * 16])

    y_d = dout("y", [NM, D])
    pcaug_d = dout("pcaug", [4, 256, 257]); ps_d = dout("ps", [4, 128, 256]); pm_d = dout("pm", [4, 1])
    pconv_d = dout("pconv", [128, 16 * 3])
    scaug_d = dout("scaug", [16, 4, 256, 257]); ss_d = dout("ss", [16, 4, 128, 256]); sm_d = dout("sm", [16, 4])
    sconv_d = dout("sconv", [128, 16 * 3 * 16])

    T = Trk(nc)
    uid = [0]

    def sb(shape, dt, name=None):
        uid[0] += 1
        return nc.alloc_sbuf_tensor(name or f"t{uid[0]}", list(shape), dt)

    ident_bf = sb([128, 128], BF16); ident_f = sb([128, 128], F32)
    maskA = sb([128, 128], F32); maskB = sb([128, 128], F32)
    rmask = sb([128, NM], BF16)
    diag16 = sb([16, 16], F32); ones16 = sb([16, 128], F32)
    cw = sb([128, 64], F32); cb = sb([128, 16], F32); bgate = sb([128, 4], F32)
    bi = sb([4, 1], F32); bff = sb([4, 1], F32); bi_row = sb([16, 4], F32); bf_row = sb([16, 4], F32)
    gna = sb([128, 8], F32); gnb = sb([128, 8], F32); flag = sb([128, 1], F32)
    wgu = sb([16, 512], BF16); wif = sb([128, 16 * 8], BF16); wbg = sb([128, 16 * 16], BF16)
    Caug = sb([128, 8 * 257], F32); Cbf = sb([128, 8 * 257], BF16)
    Sst = sb([128, 4 * 256], F32); Sbf = sb([128, 4 * 256], BF16)
    tokT = sb([128, NCH * 100], F32)
    histq = sb([128, 16 * 3], F32)
    pconv_sb = sb([128, 16 * 3], F32)
    conv_in = sb([128, 16 * 3 * 16], F32); sconv_sb = sb([128, 16 * 3 * 16], F32)
    xTp_tail = sb([128, 16 * 4], BF16)
    elast = sb([128, 4 * NCH], F32)
    mstate = sb([4, 1], F32)
    Sm = [sb([128, 128], BF16) for _ in range(2)]
    vt = [sb([128, 257], BF16) for _ in range(2)]
    vh = [sb([128, 257], BF16) for _ in range(2)]
    ktok = [sb([128, 256], BF16) for _ in range(2)]
    hnb = [sb([128, 256], BF16) for _ in range(2)] + [sb([16, 256], BF16)]
    small = [sb([128, 32], F32) for _ in range(2)] + [sb([16, 32], F32)]
    sacc_t = sb([128, 16], F32)
    stab = sb([16, 64], F32)
    qm_s = sb([128, 2 * 16 * 16], BF16)
    vm_s = sb([16, 4 * 257], BF16)
    accb = [sb([128, 512], F32) for _ in range(2)]
    decbc = sb([128, 64], F32)
    Hs = sb([16, 257], F32)
    NSL = 6
    PFD = 4
    bgT = sb([16, NM], BF16)
    stq = sb([128, 64], BF16); stk = sb([128, 64], BF16); stv = sb([16, 512], BF16)
    diagbc = sb([128, 256], BF16)
    qk_s = sb([16, 512], BF16)
    elast_s = sb([128, 4 * 16], F32)
    last16 = sb([128, 2 * NCH], F32)
    junk16 = sb([16, 256], F32)
    Y = sb([128, 16 * NM], BF16)
    QR = sb([128, 12928 + 4160], BF16)
    Q = QR[:, 0:12928]
    R = QR[:, 12928:12928 + 4160]
    remaining = nc.sbuf_bytes_remaining
    NWS = 3
    assert remaining >= 2 * (16 * NM + NWS * 8192), remaining
    XA = sb([128, 16 * NM + NWS * 8192], BF16)
    X = XA[:, 0:16 * NM]
    A = XA[:, 16 * NM:16 * NM + NWS * 8192]

    print('SBUF remaining bytes:', nc.sbuf_bytes_remaining)
    SLW = 1286
    cin = [A[:, 16384 + k * SLW:16384 + k * SLW + 514].bitcast(F32) for k in range(NSL)]
    cinb = [A[:, 16384 + k * SLW + 514:16384 + k * SLW + 771] for k in range(NSL)]
    cout = [A[:, 16384 + k * SLW + 772:16384 + k * SLW + 1286].bitcast(F32) for k in range(NSL)]
    psum = [nc.alloc_psum_tensor(f"ps{i}", [128, 512], F32) for i in range(8)]

    ps_held = set()

    def PS(hold=False):
        i = T.ps_next
        while i in ps_held:
            i = (i + 1) % 8
        T.ps_next = (i + 1) % 8
        if hold:
            ps_held.add(i)
        return i, psum[i]

    xTp = Y

    def v3(t, a, b):
        return t[:, 0:a * b].rearrange("p (a b) -> p a b", a=a)

    X3 = v3(X, 16, NM); Y3 = v3(Y, 16, NM); xTp3 = v3(xTp, 16, NP)

    def ld(dst, src, key, eng="sp"):
        T.dma(eng, "c_" + key, dst, src, w=[key])

    ld(ident_f[:], ident_d[:, :], "ident_f"); ld(ident_bf[:], ident_d[:, :], "ident_bf", "pool")
    ld(maskA[:], maskA_d[:, :], "maskA"); ld(maskB[:], maskB_d[:, :], "maskB")
    ld(rmask[:], rmask_d[:, :], "rmask", "pool"); ld(diag16[:], diag_d[:, :], "diag16")
    ld(cw[:], cw_d[:, :], "cw"); ld(cb[:], cb_d[:, :], "cb"); ld(bgate[:], bgate_d[:, :], "bgate")
    ld(bi[:], bi_d[:, :], "bi"); ld(bff[:], bf_d[:, :], "bf"); ld(bi_row[:], bi_row_d[:, :], "bi_row")
    ld(bf_row[:], bf_row_d[:, :], "bf_row"); ld(gna[:], gna_d[:, :], "gna"); ld(gnb[:], gnb_d[:, :], "gnb")
    ld(flag[:], flag_d[:, :], "flag"); ld(wgu[:], w_gu_d[:, :], "wgu", "pool")
    ld(conv_in[:], conv_in_d[:, :], "conv_in")
    ld(diagbc[:], diagbc_d[:, :], "diagbc", "pool")
    w_in_r = w_in_d.rearrange("(kc p) n -> p kc n", p=128)
    ld(v3(wif, 16, 8), w_in_r[:, :, C_I:C_I + 8], "wif", "pool")
    ld(v3(wbg, 16, 16), w_in_r[:, :, C_BG:C_BG + 16], "wbg", "pool")
    T.op("dve", lambda e: e.memset(ones16[:], 1.0), w=["ones16"])
    xTp_r = xTp_d.rearrange("(kc p) n -> p kc n", p=128)
    xT_r = xT_d.rearrange("(kc p) n -> p kc n", p=128)
    for q4 in range(4):
        T.dma("pool", f"xp{q4}", xTp3[:, 4 * q4:4 * q4 + 4, :], xTp_r[:, 4 * q4:4 * q4 + 4, :], w=["xTp"] if q4 == 3 else [f"xTp_{q4}"])
    XPK = ["xTp", "xTp_0", "xTp_1", "xTp_2"]
    XK = ["X", "X_0", "X_1", "X_2"]
    T.op("dve", lambda e: e.tensor_copy(out=v3(xTp_tail, 16, 4), in_=xTp3[:, :, NP - 4:NP]), r=XPK, w=["xTp_tail"])
    T.op("dve", lambda e: e.memset(Caug[:], 0.0), w=["Caug"])
    T.op("dve", lambda e: e.memset(Cbf[:], 0.0), w=["Cbf"])
    T.op("dve", lambda e: e.memset(Sst[:], 0.0), w=["Sst"])
    T.op("dve", lambda e: e.memset(Sbf[:], 0.0), w=["Sbf"])
    T.op("dve", lambda e: e.memset(mstate[:], 0.0), w=["mstate"])
    T.op("dve", lambda e: e.memset(histq[:], 0.0), w=["histq"])

    wslot = [0]

    wcache = {}

    def wprefetch(col0, ncols=512):
        if (col0, ncols) not in wcache:
            wcache[(col0, ncols)] = wload(col0, ncols)

    def wload(col0, ncols=512, src=None, nkc=16):
        if src is None and (col0, ncols) in wcache:
            return wcache.pop((col0, ncols))
        s = wslot[0] % len(wviews)
        wslot[0] = (s + 1) % len(wviews)
        key, base = wviews[s]
        view = base[:, 0:nkc * ncols].rearrange("p (a b) -> p a b", a=nkc)
        srcr = (src if src is not None else w_in_r)
        half = nkc // 2
        T.dma("pool", f"w{key}a", view[:, 0:half, :], srcr[:, 0:half, col0:col0 + ncols], w=[key + "a"])
        T.dma("pool", f"w{key}b", view[:, half:nkc, :], srcr[:, half:nkc, col0:col0 + ncols], w=[key + "b"])
        return view, [key + "a", key + "b"]

    WV3 = [("W0", A[:, 0:8192]), ("W1", A[:, 8192:16384]), ("W2", A[:, 16384:24576])]
    WV2 = WV3[0:2]
    WV4 = WV2 + [("W3", QR[:, 0:8192]), ("W4", QR[:, 8192:16384])]
    wviews = list(WV3)

    TB_MAIN = [(0, 512), (512, 512), (1024, 16)]
    TB_PRE = [(0, 512), (512, 512)]

    def fm_proj(Wv, Wk, cb_i, xsrc, xkeys, t0, n, M=128, c0=None):
        i, ps = PS()
        lo = cb_i * 128 if c0 is None else c0
        for kc in range(16):
            T.op("pe", lambda e, kc=kc: e.matmul(ps[0:M, 0:n], Wv[:, kc, lo:lo + M], xsrc[:, kc, t0:t0 + n],
                                                 start=(kc == 0), stop=(kc == 15)),
                 r=Wk + xkeys, w=[("ps", i)])
        return i, ps

    def mlstm_gates(xsrc, xkeys, is_main, scr):
        Wv = v3(wif, 16, 8)
        G = [scr[:, k * 2048:(k + 1) * 2048].bitcast(F32) for k in range(6)]
        itil, logf, bcs, gg, big, tmp = G
        for (t0, n) in TB_PRE:
            for which, dst in ((0, itil), (1, logf)):
                i, ps = PS()
                for kc in range(16):
                    T.op("pe", lambda e, kc=kc: e.matmul(ps[0:4, 0:n], Wv[:, kc, which * 4:which * 4 + 4], xsrc[:, kc, t0:t0 + n],
                                                         start=(kc == 0), stop=(kc == 15)), r=["wif"] + xkeys, w=[("ps", i)])
                if which == 0:
                    T.op("dve", lambda e: e.tensor_scalar(out=dst[0:4, t0:t0 + n], in0=ps[0:4, 0:n], scalar1=bi[:, 0:1], scalar2=None, op0=ALU.add),
                         r=[("ps", i), "bi"], w=["g_itil"])
                else:
                    T.op("act", lambda e: e.activation(out=dst[0:4, t0:t0 + n], in_=ps[0:4, 0:n], func=AF.Sigmoid, bias=bff[:, 0:1]),
                         r=[("ps", i), "bf"], w=["g_logf"])
        T.op("act", lambda e: e.activation(out=logf[0:4, 0:NP], in_=logf[0:4, 0:NP], func=AF.Ln), r=["g_logf"], w=["g_logf"])
        T.op("dve", lambda e: e.tensor_tensor_scan(out=bcs[0:4, 0:NP], data0=rmask[0:4, 0:NP], data1=logf[0:4, 0:NP], initial=0.0,
                                                   op0=ALU.mult, op1=ALU.add), r=["g_logf", "rmask"], w=["g_b"])
        T.op("dve", lambda e: e.tensor_tensor(out=gg[0:4, 0:NP], in0=itil[0:4, 0:NP], in1=bcs[0:4, 0:NP], op=ALU.subtract),
             r=["g_itil", "g_b"], w=["g_g"])
        st = tmp
        T.op("dve", lambda e: e.tensor_reduce(out=st[0:4, 0:NCH], in_=gg[0:4, 0:NP].rearrange("p (c t) -> p c t", c=NCH), axis=AX.X, op=ALU.max),
             r=["g_g"], w=["g_st"])
        T.op("dve", lambda e: e.tensor_copy(out=st[0:4, 8:16], in_=bcs[0:4, 127:NP:128]), r=["g_b", "g_st"], w=["g_st"])
        T.op("dve", lambda e: e.tensor_tensor_scan(out=st[0:4, 16:24], data0=st[0:4, 0:8], data1=st[0:4, 8:16], initial=mstate[0:4, 0:1],
                                                   op0=ALU.max, op1=ALU.add), r=["g_st", "mstate"], w=["g_st"])
        T.op("dve", lambda e: e.tensor_copy(out=st[0:4, 24:25], in_=mstate[0:4, 0:1]), r=["g_st", "mstate"], w=["g_st"])
        T.op("dve", lambda e: e.tensor_copy(out=st[0:4, 25:32], in_=st[0:4, 16:23]), r=["g_st"], w=["g_st"])
        T.op("dve", lambda e: e.tensor_copy(out=mstate[0:4, 0:1], in_=st[0:4, 23:24]), r=["g_st"], w=["mstate"])
        T.op("dve", lambda e: e.tensor_tensor(out=st[0:4, 40:48], in0=st[0:4, 8:16], in1=st[0:4, 24:32], op=ALU.add), r=["g_st"], w=["g_st"])
        T.op("dve", lambda e: e.tensor_tensor(out=st[0:4, 40:48], in0=st[0:4, 40:48], in1=st[0:4, 16:24], op=ALU.subtract), r=["g_st"], w=["g_st"])
        T.op("act", lambda e: e.activation(out=st[0:4, 32:40], in_=st[0:4, 40:48], func=AF.Exp), r=["g_st"], w=["g_st"])
        big3 = big[:, 0:NP].rearrange("p (c t) -> p c t", c=NCH)
        T.op("dve", lambda e: e.memset(big[0:100, 0:NP], 0.0), w=["g_big"])
        mprev_b = st[0:4, 24:32].unsqueeze(2).to_broadcast([4, NCH, 128])
        dec_b = st[0:4, 32:40].unsqueeze(2).to_broadcast([4, NCH, 128])
        gg3 = gg[0:4, 0:NP].rearrange("p (c t) -> p c t", c=NCH)
        b3 = bcs[0:4, 0:NP].rearrange("p (c t) -> p c t", c=NCH)
        T.op("dve", lambda e: e.tensor_tensor(out=gg3, in0=gg3, in1=mprev_b, op=ALU.subtract), r=["g_g", "g_st"], w=["g_g"])
        T.op("act", lambda e: e.activation(out=big[0:4, 0:NP], in_=gg[0:4, 0:NP], func=AF.Exp), r=["g_g", "g_big"], w=["g_big"])
        T.op("dve", lambda e: e.scalar_tensor_tensor(out=big3[32:36], in0=big3[0:4], scalar=1.0 / 16.0, in1=dec_b, op0=ALU.mult, op1=ALU.mult),
             r=["g_big", "g_st"], w=["g_big"])
        T.op("dve", lambda e: e.tensor_tensor(out=b3, in0=b3, in1=mprev_b, op=ALU.add), r=["g_b", "g_st"], w=["g_b"])
        T.op("act", lambda e: e.activation(out=big[64:68, 0:NP], in_=bcs[0:4, 0:NP], func=AF.Exp, scale=-1.0), r=["g_b", "g_big"], w=["g_big"])
        T.op("dve", lambda e: e.tensor_copy(out=big3[96:100], in_=dec_b), r=["g_st", "g_big"], w=["g_big"])
        tok3 = v3(tokT, NCH, 100)
        for c4 in range(0, NCH, 4):
            i, ps = PS()
            for c in range(c4, c4 + 4):
                T.op("pe", lambda e, c=c: e.transpose(ps[:, (c - c4) * 100:(c - c4) * 100 + 100], big[0:100, c * 128:(c + 1) * 128], ident_f[0:100, 0:100]),
                     r=["g_big", "ident_f"], w=[("ps", i)])
            T.op("act", lambda e: e.activation(out=tokT[:, c4 * 100:(c4 + 4) * 100], in_=ps[:, 0:400], func=AF.Copy), r=[("ps", i)], w=["tokT"])
        return tok3

    pre_rot = [0]
    acc_rot = [0]
    preb = [R[:, k * 1040:(k + 1) * 1040].bitcast(F32) for k in range(3)]

    def conv_block(i, ps, blk, t0, n, dst, dstkey, is_main, tbidx, save_pconv):
        k = pre_rot[0]; pre_rot[0] = (k + 1) % 3
        P = preb[k]; pk = f"pre{k}"
        T.op("act", lambda e: e.activation(out=P[:, 3:3 + n], in_=ps[:, 0:n], func=AF.Copy), r=[("ps", i)], w=[pk])
        if tbidx == 0:
            T.op("dve", lambda e: e.tensor_copy(out=P[:, 0:3], in_=histq[:, blk * 3:blk * 3 + 3]), r=["histq", pk], w=[pk])
        else:
            kp = (k + 2) % 3
            T.op("dve", lambda e: e.tensor_copy(out=P[:, 0:3], in_=preb[kp][:, 512:515]), r=[f"pre{kp}", pk], w=[pk])
        ka = acc_rot[0]; acc_rot[0] = (ka + 1) % 2
        acc = hn_acc[ka]
        ak = f"acc{ka}"
        c4 = blk * 4
        T.op("dve", lambda e: e.tensor_scalar(out=acc[:, 0:n], in0=P[:, 3:3 + n], scalar1=cw[:, c4 + 3:c4 + 4], scalar2=cb[:, blk:blk + 1],
                                              op0=ALU.mult, op1=ALU.add), r=[pk, "cw", "cb"], w=[ak])
        for j in (2, 1, 0):
            T.op("dve", lambda e, j=j: e.scalar_tensor_tensor(out=acc[:, 0:n], in0=P[:, j:j + n], scalar=cw[:, c4 + j:c4 + j + 1], in1=acc[:, 0:n],
                                                              op0=ALU.mult, op1=ALU.add), r=[pk, ak], w=[ak])
        T.op("act", lambda e: e.activation(out=dst, in_=acc[:, 0:n], func=AF.Silu), r=[ak], w=[dstkey])
        if tbidx == 1:
            if save_pconv:
                T.op("dve", lambda e: e.tensor_copy(out=pconv_sb[:, blk * 3:blk * 3 + 3], in_=P[:, 512:515]), r=[pk], w=["pconv_sb"])
            else:
                T.op("dve", lambda e: e.tensor_scalar(out=histq[:, blk * 3:blk * 3 + 3], in0=P[:, 512:515], scalar1=flag[:, 0:1], scalar2=None, op0=ALU.mult),
                     r=[pk, "flag"], w=["histq"])


    hn_acc = accb

    conv_in4 = conv_in[:, :].rearrange("p (b j s) -> p b j s", b=16, j=3)
    sconv4 = sconv_sb[:, :].rearrange("p (b j s) -> p b j s", b=16, j=3)

    def conv_block_sample(i, ps, blk, dst, dstkey):
        acc = sacc_t
        c4 = blk * 4
        T.op("dve", lambda e: e.tensor_scalar(out=acc[:, 0:16], in0=ps[:, 0:16], scalar1=cw[:, c4 + 3:c4 + 4], scalar2=cb[:, blk:blk + 1],
                                              op0=ALU.mult, op1=ALU.add), r=[("ps", i)], w=["sacc"])
        for j in (2, 1, 0):
            T.op("dve", lambda e, j=j: e.scalar_tensor_tensor(out=acc[:, 0:16], in0=conv_in4[:, blk, j, :], scalar=cw[:, c4 + j:c4 + j + 1], in1=acc[:, 0:16],
                                                              op0=ALU.mult, op1=ALU.add), r=["conv_in", "sacc"], w=["sacc"])
        T.op("act", lambda e: e.activation(out=dst, in_=acc[:, 0:16], func=AF.Silu), r=["sacc"], w=[dstkey])
        T.op("act", lambda e: e.activation(out=sconv4[:, blk, 2, :], in_=ps[:, 0:16], func=AF.Copy), r=[("ps", i)], w=["sconv_sb"])
        T.op("dve", lambda e: e.tensor_copy(out=sconv4[:, blk, 0:2, :], in_=conv_in4[:, blk, 1:3, :]), r=["conv_in"], w=["sconv_sb"])

    qT = Q[:, 0:4 * NM].rearrange("p (a b) -> p a b", a=4)
    kT = Q[:, 4 * NM:8 * NM].rearrange("p (a b) -> p a b", a=4)
    vtok = Q[:, 8 * NM:8 * NM + 9 * 512].rearrange("p (a b) -> p a b", a=9)
    gq = Q[:, 0:2 * NM].rearrange("p (a b) -> p a b", a=2)
    gk = Q[:, 2 * NM:4 * NM].rearrange("p (a b) -> p a b", a=2)
    gkh = Q[:, 4 * NM:6 * NM].rearrange("p (a b) -> p a b", a=2)

    def post_mlstm(Hap, Hkeys, theta_ap, theta_keys, npart, k, dst_fn, mid=None):
        sm = small[k]; sk = f"small{k}"
        T.op("dve", lambda e: e.tensor_scalar(out=sm[0:npart, 6:7], in0=Hap[:, 256:257], scalar1=-1.0, scalar2=None, op0=ALU.mult),
             r=Hkeys, w=[sk])
        T.op("dve", lambda e: e.scalar_tensor_tensor(out=sm[0:npart, 0:1], in0=Hap[:, 256:257], scalar=theta_ap, in1=sm[0:npart, 6:7], op0=ALU.max, op1=ALU.max),
             r=Hkeys + theta_keys + [sk], w=[sk])
        T.op("dve", lambda e: e.bn_stats(out=sm[0:npart, 8:14], in_=Hap[:, 0:256]), r=Hkeys + [sk], w=[sk])
        T.op("dve", lambda e: e.bn_aggr(out=sm[0:npart, 2:4], in_=sm[0:npart, 8:14]), r=[sk], w=[sk])
        T.op("dve", lambda e: e.tensor_tensor(out=sm[0:npart, 1:2], in0=sm[0:npart, 0:1], in1=sm[0:npart, 0:1], op=ALU.mult), r=[sk], w=[sk])
        T.op("dve", lambda e: e.scalar_tensor_tensor(out=sm[0:npart, 1:2], in0=sm[0:npart, 1:2], scalar=EPS, in1=sm[0:npart, 3:4], op0=ALU.mult, op1=ALU.add),
             r=[sk], w=[sk])
        T.op("act", lambda e: e.activation(out=sm[0:npart, 4:5], in_=sm[0:npart, 1:2], func=AF.Ln), r=[sk], w=[sk])
        T.op("act", lambda e: e.activation(out=sm[0:npart, 5:6], in_=sm[0:npart, 4:5], func=AF.Exp, scale=-0.5), r=[sk], w=[sk])
        if mid is not None:
            mid()
        T.op("dve", lambda e: e.tensor_scalar(out=hnb[k][0:npart, :], in0=Hap[:, 0:256], scalar1=sm[0:npart, 2:3], scalar2=sm[0:npart, 5:6],
                                              op0=ALU.subtract, op1=ALU.mult), r=Hkeys + [sk], w=[f"hnb{k}"])
        dst_fn(k)

    def to_Y(k, npart, blk0, t0):
        i, ps = PS()
        psb = ps[:, :].bitcast(BF16)
        for dh in range(2):
            T.op("pe", lambda e, dh=dh: e.transpose(psb[:, dh * 128:dh * 128 + npart], hnb[k][0:npart, dh * 128:(dh + 1) * 128], ident_bf[0:npart, 0:npart]),
                 r=[f"hnb{k}", "ident_bf"], w=[("ps", i)])
        T.op("act", lambda e: e.activation(out=Y3[:, blk0:blk0 + 2, t0:t0 + npart], in_=psb[:, 0:256].rearrange("p (a b) -> p a b", a=2)[:, :, 0:npart], func=AF.Copy),
             r=[("ps", i)], w=["Y"])

    def mlstm_segment(hp, is_main, tok3, bg=None, nxt=None, bg2=None, bgn=2):
        def tick():
            if bg2 is not None:
                bg2(bgn)
        xsrc = X3 if is_main else xTp3
        xkeys = XK if is_main else XPK
        tbs = TB_MAIN if is_main else TB_PRE
        heads = (2 * hp, 2 * hp + 1)
        if is_main:
            Wq, Wqk = wload(C_Q + hp * 512)
            for cbi in range(4):
                blk = hp * 4 + cbi
                i, ps = PS()
                for kc in range(16):
                    T.op("pe", lambda e, kc=kc: e.matmul(ps[:, 0:4], Wq[:, kc, cbi * 128:(cbi + 1) * 128], v3(xTp_tail, 16, 4)[:, kc, :],
                                                         start=(kc == 0), stop=(kc == 15)), r=Wqk + ["xTp_tail"], w=[("ps", i)])
                T.op("dve", lambda e: e.tensor_scalar(out=histq[:, blk * 3:blk * 3 + 3], in0=ps[:, 1:4], scalar1=flag[:, 0:1], scalar2=None, op0=ALU.mult),
                     r=[("ps", i), "flag"], w=["histq"])
                for tbi, (t0, n) in enumerate(tbs):
                    i, ps = fm_proj(Wq, Wqk, cbi, xsrc, xkeys, t0, n)
                    if n == 16:
                        conv_block_sample(i, ps, blk, qT[:, cbi, t0:t0 + n], "qT")
                    else:
                        conv_block(i, ps, blk, t0, n, qT[:, cbi, t0:t0 + n], "qT", is_main, tbi, True)
                    tick()
        Wk_, Wkk = wload(C_K + hp * 512)
        for cbi in range(4):
            blk = 8 + hp * 4 + cbi
            for tbi, (t0, n) in enumerate(tbs):
                i, ps = fm_proj(Wk_, Wkk, cbi, xsrc, xkeys, t0, n)
                if n == 16:
                    conv_block_sample(i, ps, blk, kT[:, cbi, t0:t0 + n], "kT")
                else:
                    conv_block(i, ps, blk, t0, n, kT[:, cbi, t0:t0 + n], "kT", is_main, tbi, is_main)
                tick()
        Wv_, Wvk = wload(C_V + hp * 512)
        ntile = 9 if is_main else 8
        for tl in range(ntile):
            m = 128 if tl < 8 else 16
            i, ps = PS()
            for kc in range(16):
                T.op("pe", lambda e, kc=kc: e.matmul(ps[0:m, 0:512], xsrc[:, kc, tl * 128:tl * 128 + m], Wv_[:, kc, :], start=(kc == 0), stop=(kc == 15)),
                     r=Wvk + xkeys, w=[("ps", i)])
            T.op("act", lambda e: e.activation(out=vtok[0:m, tl, :], in_=ps[0:m, 0:512], func=AF.Copy), r=[("ps", i)], w=["vtok"])
            tick()
        if nxt is not None:
            nxt()
        items = [(c, hl, h) for c in range(NCH) for hl, h in enumerate(heads)]
        held = {}

        def stA(i):
            c, hl, h = items[i]; k = i % 2
            ts = slice(c * 128, (c + 1) * 128)
            if is_main:
                i_s, ps_s = PS()
                for dh in range(2):
                    T.op("pe", lambda e, dh=dh: e.matmul(ps_s[:, 0:128], kT[:, hl * 2 + dh, ts], qT[:, hl * 2 + dh, ts], start=(dh == 0), stop=(dh == 1)),
                         r=["kT", "qT"], w=[("ps", i_s)])
            i_t, ps_t = PS()
            pst = ps_t[:, :].bitcast(BF16)
            for dh in range(2):
                T.op("pe", lambda e, dh=dh: e.transpose(pst[:, dh * 128:(dh + 1) * 128], kT[:, hl * 2 + dh, ts], ident_bf[:]), r=["kT", "ident_bf"], w=[("ps", i_t)])
            if is_main:
                T.op("dve", lambda e: e.tensor_tensor(out=Sm[k][:], in0=ps_s[:, 0:128], in1=maskA[:], op=ALU.mult), r=[("ps", i_s), "maskA"], w=[f"Sm{k}"])
            T.op("act", lambda e: e.activation(out=ktok[k][:], in_=pst[:, 0:256], func=AF.Copy), r=[("ps", i_t)], w=[f"ktok{k}"])
            if is_main:
                T.op("pool", lambda e: e.tensor_scalar(out=vt[k][:, 0:256], in0=vtok[:, c, hl * 256:(hl + 1) * 256], scalar1=tok3[:, c, h:h + 1], scalar2=0.0, op0=ALU.mult, op1=ALU.add),
                     r=["vtok", "tokT"], w=[f"vt{k}"])
                T.op("pool", lambda e: e.tensor_copy(out=vt[k][:, 256:257], in_=tok3[:, c, h:h + 1]), r=["tokT"], w=[f"vt{k}b"])
            T.op("pool", lambda e: e.tensor_scalar(out=vh[k][:, 0:256], in0=vtok[:, c, hl * 256:(hl + 1) * 256], scalar1=tok3[:, c, 32 + h:33 + h], scalar2=0.0, op0=ALU.mult, op1=ALU.add),
                 r=["vtok", "tokT"], w=[f"vh{k}"])
            T.op("pool", lambda e: e.tensor_copy(out=vh[k][:, 256:257], in_=tok3[:, c, 32 + h:33 + h]), r=["tokT"], w=[f"vh{k}b"])

        def stB(i):
            c, hl, h = items[i]; k = i % 2
            ts = slice(c * 128, (c + 1) * 128)
            if is_main:
                i_h, ps_h = PS(hold=True)
                T.op("pe", lambda e: e.matmul(ps_h[:, 0:257], Sm[k][:], vt[k][:], start=True, stop=False), r=[f"Sm{k}", f"vt{k}", f"vt{k}b"], w=[("ps", i_h)])
                for dh in range(2):
                    T.op("pe", lambda e, dh=dh: e.matmul(ps_h[:, 0:257], qT[:, hl * 2 + dh, ts], Cbf[:, (h * 2 + dh) * 257:(h * 2 + dh + 1) * 257], start=False, stop=(dh == 1)),
                         r=["qT", f"Cbf{h}"], w=[("ps", i_h)])
            for dh in range(2):
                i_u, ps_u = PS()
                sl = slice((h * 2 + dh) * 257, (h * 2 + dh + 1) * 257)
                T.op("pe", lambda e, dh=dh: e.matmul(ps_u[:, 0:257], ktok[k][:, dh * 128:(dh + 1) * 128], vh[k][:], start=True, stop=True),
                     r=[f"ktok{k}", f"vh{k}", f"vh{k}b"], w=[("ps", i_u)])
                T.op("dve", lambda e, sl=sl: e.scalar_tensor_tensor(out=Caug[:, sl], in0=Caug[:, sl], scalar=tok3[:, c, 96 + h:97 + h], in1=ps_u[:, 0:257], op0=ALU.mult, op1=ALU.add),
                     r=[("ps", i_u), "tokT", f"Caug{h}"], w=[f"Caug{h}"])
                if is_main:
                    T.op("act", lambda e, sl=sl: e.activation(out=Cbf[:, sl], in_=Caug[:, sl], func=AF.Copy), r=[f"Caug{h}"], w=[f"Cbf{h}"])
            if is_main:
                post_mlstm(ps_h[:, 0:257], [("ps", i_h)], tok3[:, c, 64 + h:65 + h], ["tokT"], 128, k, lambda k: None,
                           mid=(lambda: bg(4)) if bg is not None else None)
                ps_held.discard(i_h)

        def stC(i):
            c, hl, h = items[i]; k = i % 2
            to_Y(k, 128, h * 2, c * 128)

        NI = len(items)
        stA(0)
        for i in range(NI):
            if i + 1 < NI:
                stA(i + 1)
            stB(i)
            if is_main and i >= 1:
                stC(i - 1)
            if bg is not None and not is_main:
                bg(4)
        if is_main:
            stC(NI - 1)
        return

    def post_gla(Oap, Okeys, npart, k, dst_fn, mid=None):
        sm = small[k]; sk = f"small{k}"
        T.op("dve", lambda e: e.bn_stats(out=sm[0:npart, 8:14], in_=Oap), r=Okeys + [sk], w=[sk])
        T.op("dve", lambda e: e.bn_aggr(out=sm[0:npart, 2:4], in_=sm[0:npart, 8:14]), r=[sk], w=[sk])
        T.op("dve", lambda e: e.tensor_tensor(out=sm[0:npart, 1:2], in0=sm[0:npart, 2:3], in1=sm[0:npart, 2:3], op=ALU.mult), r=[sk], w=[sk])
        T.op("dve", lambda e: e.scalar_tensor_tensor(out=sm[0:npart, 1:2], in0=sm[0:npart, 1:2], scalar=EPS, in1=sm[0:npart, 3:4], op0=ALU.add, op1=ALU.add),
             r=[sk], w=[sk])
        T.op("act", lambda e: e.activation(out=sm[0:npart, 4:5], in_=sm[0:npart, 1:2], func=AF.Ln), r=[sk], w=[sk])
        T.op("act", lambda e: e.activation(out=sm[0:npart, 5:6], in_=sm[0:npart, 4:5], func=AF.Exp, scale=-0.5), r=[sk], w=[sk])
        if mid is not None:
            mid()
        T.op("dve", lambda e: e.tensor_scalar(out=hnb[k][0:npart, :], in0=Oap, scalar1=sm[0:npart, 5:6], scalar2=None, op0=ALU.mult), r=Okeys + [sk], w=[f"hnb{k}"])
        dst_fn(k)

    def gla_bg(xsrc, xkeys, tbs):
        Wv = v3(wbg, 16, 16)
        for (t0, n) in tbs:
            i, ps = PS()
            for kc in range(16):
                T.op("pe", lambda e, kc=kc: e.matmul(ps[0:16, 0:n], Wv[:, kc, 0:16], xsrc[:, kc, t0:t0 + n], start=(kc == 0), stop=(kc == 15)),
                     r=["wbg"] + xkeys, w=[("ps", i)])
            T.op("act", lambda e: e.activation(out=bgT[0:16, t0:t0 + n], in_=ps[0:16, 0:n], func=AF.Copy), r=[("ps", i)], w=["bgT"])

    gvtok = vtok
    stq3 = stq[:, :].rearrange("p (a b) -> p a b", a=4)
    stk3 = stk[:, :].rearrange("p (a b) -> p a b", a=4)

    def stash_mlstm():
        T.op("act", lambda e: e.activation(out=stq3, in_=qT[:, :, NP:NM], func=AF.Copy), r=["qT"], w=["stq"])
        T.op("act", lambda e: e.activation(out=stk3, in_=kT[:, :, NP:NM], func=AF.Copy), r=["kT"], w=["stk"])
        T.op("act", lambda e: e.activation(out=stv[:, :], in_=vtok[0:16, 8, :], func=AF.Copy), r=["vtok"], w=["stv"])

    def stash_gla():
        T.op("act", lambda e: e.activation(out=stq3[:, 0:2, :], in_=gq[:, :, NP:NM], func=AF.Copy), r=["gq"], w=["stq"])
        T.op("act", lambda e: e.activation(out=stk3[:, 0:2, :], in_=gk[:, :, NP:NM], func=AF.Copy), r=["gk"], w=["stk"])
        T.op("act", lambda e: e.activation(out=stk3[:, 2:4, :], in_=gkh[:, :, NP:NM], func=AF.Copy), r=["gkh"], w=["stk"])
        T.op("act", lambda e: e.activation(out=stv[:, :], in_=gvtok[0:16, 8, :], func=AF.Copy), r=["vtok"], w=["stv"])
    bcb = [R[:, k * 2080:(k + 1) * 2080].bitcast(F32) for k in range(2)]

    def gla_segment(hp, is_main, bg=None, nxt=None, bg2=None, bgn=2):
        def tick():
            if bg2 is not None:
                bg2(bgn)
        xsrc = X3 if is_main else xTp3
        xkeys = XK if is_main else XPK
        tbs = TB_MAIN if is_main else TB_PRE
        NT = NM if is_main else NP
        heads = (2 * hp, 2 * hp + 1)
        for hl, h in enumerate(heads):
            bcs = bcb[hl]; bk = f"bcs{hl}"
            for (t0, n) in tbs:
                i, ps = PS()
                T.op("pe", lambda e: e.matmul(ps[:, 0:n], wgu[0:16, h * 128:(h + 1) * 128], bgT[0:16, t0:t0 + n], start=True, stop=True),
                     r=["wgu", "bgT"], w=[("ps", i)])
                T.op("act", lambda e: e.activation(out=bcs[:, t0:t0 + n], in_=ps[:, 0:n], func=AF.Sigmoid, bias=bgate[:, h:h + 1]), r=[("ps", i), "bgate"], w=[bk])
            T.op("act", lambda e: e.activation(out=bcs[:, 0:NT], in_=bcs[:, 0:NT], func=AF.Ln), r=[bk], w=[bk])
            T.op("dve", lambda e: e.tensor_tensor_scan(out=bcs[:, 0:NT], data0=rmask[:, 0:NT], data1=bcs[:, 0:NT], initial=0.0, op0=ALU.mult, op1=ALU.add),
                 r=[bk, "rmask"], w=[bk])
            T.op("dve", lambda e: e.tensor_scalar(out=last16[:, hl * NCH:(hl + 1) * NCH], in0=bcs[:, 127:NP:128], scalar1=1.0 / 16.0, scalar2=None, op0=ALU.mult),
                 r=[bk], w=["last16"])
            T.op("act", lambda e: e.activation(out=elast[:, h * NCH:(h + 1) * NCH], in_=bcs[:, 127:NP:128], func=AF.Exp, scale=1.0 / 16.0), r=[bk], w=["elast"])
            if is_main:
                T.op("act", lambda e: e.activation(out=elast_s[:, h * 16:(h + 1) * 16], in_=bcs[:, NP:NM], func=AF.Exp, scale=1.0 / 16.0), r=[bk], w=["elast_s"])
        if is_main:
            Wq, Wqk = wload(C_BQ + hp * 256, ncols=256)
            for hl, h in enumerate(heads):
                for (t0, n) in tbs:
                    i, ps = fm_proj(Wq, Wqk, hl, xsrc, xkeys, t0, n)
                    k = acc_rot[0]; acc_rot[0] = (k + 1) % 2
                    T.op("act", lambda e: e.activation(out=accb[k][:, 0:n], in_=bcb[hl][:, t0:t0 + n], func=AF.Exp, scale=1.0 / 16.0), r=[f"bcs{hl}"], w=[f"acc{k}"])
                    T.op("dve", lambda e: e.scalar_tensor_tensor(out=gq[:, hl, t0:t0 + n], in0=ps[:, 0:n], scalar=float(128 ** -0.5), in1=accb[k][:, 0:n], op0=ALU.mult, op1=ALU.mult),
                         r=[("ps", i), f"acc{k}"], w=["gq"])
                    tick()
        Wk_, Wkk = wload(C_BK + hp * 256, ncols=256)
        for hl, h in enumerate(heads):
            for (t0, n) in tbs:
                i, ps = fm_proj(Wk_, Wkk, hl, xsrc, xkeys, t0, n)
                if is_main:
                    k = acc_rot[0]; acc_rot[0] = (k + 1) % 2
                    T.op("act", lambda e: e.activation(out=accb[k][:, 0:n], in_=bcb[hl][:, t0:t0 + n], func=AF.Exp, scale=-1.0 / 16.0), r=[f"bcs{hl}"], w=[f"acc{k}"])
                    T.op("dve", lambda e: e.tensor_tensor(out=gk[:, hl, t0:t0 + n], in0=ps[:, 0:n], in1=accb[k][:, 0:n], op=ALU.mult), r=[("ps", i), f"acc{k}"], w=["gk"])
                if n == 16:
                    T.op("act", lambda e: e.activation(out=gkh[:, hl, t0:t0 + n], in_=ps[:, 0:n], func=AF.Copy), r=[("ps", i)], w=["gkh"])
                else:
                    k = acc_rot[0]; acc_rot[0] = (k + 1) % 2
                    for cc in range(n // 128):
                        c = (t0 + cc * 128) // 128
                        T.op("act", lambda e, cc=cc, c=c: e.activation(out=accb[k][:, cc * 128:(cc + 1) * 128], in_=bcb[hl][:, t0 + cc * 128:t0 + (cc + 1) * 128], func=AF.Exp,
                                                                       scale=-1.0 / 16.0, bias=last16[:, hl * NCH + c:hl * NCH + c + 1]), r=[f"bcs{hl}", "last16"], w=[f"acc{k}"])
                    T.op("dve", lambda e: e.tensor_tensor(out=gkh[:, hl, t0:t0 + n], in0=ps[:, 0:n], in1=accb[k][:, 0:n], op=ALU.mult), r=[("ps", i), f"acc{k}"], w=["gkh"])
                tick()
        Wv_, Wvk = wload(C_BV + hp * 512)
        ntile = 9 if is_main else 8
        for tl in range(ntile):
            m = 128 if tl < 8 else 16
            i, ps = PS()
            for kc in range(16):
                T.op("pe", lambda e, kc=kc: e.matmul(ps[0:m, 0:512], xsrc[:, kc, tl * 128:tl * 128 + m], Wv_[:, kc, :], start=(kc == 0), stop=(kc == 15)),
                     r=Wvk + xkeys, w=[("ps", i)])
            T.op("act", lambda e: e.activation(out=gvtok[0:m, tl, :], in_=ps[0:m, 0:512], func=AF.Copy), r=[("ps", i)], w=["vtok"])
            tick()
        if nxt is not None:
            nxt()
        items = [(c, hl, h) for c in range(NCH) for hl, h in enumerate(heads)]

        def stA(i):
            c, hl, h = items[i]; k = i % 2
            ts = slice(c * 128, (c + 1) * 128)
            if is_main:
                i_s, ps_s = PS()
                T.op("pe", lambda e: e.matmul(ps_s[:, 0:128], gk[:, hl, ts], gq[:, hl, ts], start=True, stop=True), r=["gk", "gq"], w=[("ps", i_s)])
            i_t, ps_t = PS()
            pst = ps_t[:, :].bitcast(BF16)
            T.op("pe", lambda e: e.transpose(pst[:, 0:128], gkh[:, hl, ts], ident_bf[:]), r=["gkh", "ident_bf"], w=[("ps", i_t)])
            if is_main:
                T.op("dve", lambda e: e.tensor_tensor(out=Sm[k][:], in0=ps_s[:, 0:128], in1=maskB[:], op=ALU.mult), r=[("ps", i_s), "maskB"], w=[f"Sm{k}"])
            T.op("act", lambda e: e.activation(out=ktok[k][:, 0:128], in_=pst[:, 0:128], func=AF.Copy), r=[("ps", i_t)], w=[f"ktok{k}"])

        def stB(i):
            c, hl, h = items[i]; k = i % 2
            ts = slice(c * 128, (c + 1) * 128)
            ssl = slice(h * 256, (h + 1) * 256)
            if is_main:
                i_h, ps_h = PS(hold=True)
                T.op("pe", lambda e: e.matmul(ps_h[:, 0:256], Sm[k][:], gvtok[:, c, hl * 256:(hl + 1) * 256], start=True, stop=False), r=[f"Sm{k}", "vtok"], w=[("ps", i_h)])
                T.op("pe", lambda e: e.matmul(ps_h[:, 0:256], gq[:, hl, ts], Sbf[:, ssl], start=False, stop=True), r=["gq", f"Sbf{h}"], w=[("ps", i_h)])
            i_u, ps_u = PS()
            T.op("pe", lambda e: e.matmul(ps_u[:, 0:256], ktok[k][:, 0:128], gvtok[:, c, hl * 256:(hl + 1) * 256], start=True, stop=True), r=[f"ktok{k}", "vtok"], w=[("ps", i_u)])
            T.op("dve", lambda e: e.scalar_tensor_tensor(out=Sst[:, ssl], in0=Sst[:, ssl], scalar=elast[:, h * NCH + c:h * NCH + c + 1], in1=ps_u[:, 0:256], op0=ALU.mult, op1=ALU.add),
                 r=[("ps", i_u), "elast", f"Sst{h}"], w=[f"Sst{h}"])
            if is_main:
                T.op("act", lambda e: e.activation(out=Sbf[:, ssl], in_=Sst[:, ssl], func=AF.Copy), r=[f"Sst{h}"], w=[f"Sbf{h}"])
            if is_main:
                post_gla(ps_h[:, 0:256], [("ps", i_h)], 128, k, lambda k: None, mid=(lambda: bg(2)) if bg is not None else None)
                ps_held.discard(i_h)

        def stC(i):
            c, hl, h = items[i]; k = i % 2
            to_Y(k, 128, 8 + h * 2, c * 128)

        NI = len(items)
        stA(0)
        for i in range(NI):
            if i + 1 < NI:
                stA(i + 1)
            stB(i)
            if is_main and i >= 1:
                stC(i - 1)
            if bg is not None and not is_main:
                bg(2)
        if is_main:
            stC(NI - 1)

    def sample_gates():
        Wv = v3(wif, 16, 8)
        T.dma("sp", "c_m_in", stab[:, 8:12], m_in_d[:, :], w=["stab_m"])
        i, ps = PS()
        for kc in range(16):
            T.op("pe", lambda e, kc=kc: e.matmul(ps[0:16, 0:8], X3[:, kc, NP:NM], Wv[:, kc, 0:8], start=(kc == 0), stop=(kc == 15)), r=["wif"] + XK, w=[("ps", i)])
        T.op("dve", lambda e: e.tensor_tensor(out=stab[:, 0:4], in0=ps[0:16, 0:4], in1=bi_row[:], op=ALU.add), r=[("ps", i), "bi_row"], w=["stab"])
        T.op("dve", lambda e: e.tensor_tensor(out=stab[:, 4:8], in0=ps[0:16, 4:8], in1=bf_row[:], op=ALU.add), r=[("ps", i), "bf_row", "stab"], w=["stab"])
        T.op("act", lambda e: e.activation(out=stab[:, 4:8], in_=stab[:, 4:8], func=AF.Sigmoid), r=["stab"], w=["stab"])
        T.op("act", lambda e: e.activation(out=stab[:, 4:8], in_=stab[:, 4:8], func=AF.Ln), r=["stab"], w=["stab"])
        T.op("dve", lambda e: e.tensor_tensor(out=stab[:, 12:16], in0=stab[:, 4:8], in1=stab[:, 8:12], op=ALU.add), r=["stab", "stab_m"], w=["stab"])
        T.op("dve", lambda e: e.tensor_tensor(out=stab[:, 16:20], in0=stab[:, 12:16], in1=stab[:, 0:4], op=ALU.max), r=["stab"], w=["stab"])
        T.op("dve", lambda e: e.tensor_tensor(out=stab[:, 20:24], in0=stab[:, 0:4], in1=stab[:, 16:20], op=ALU.subtract), r=["stab"], w=["stab"])
        T.op("dve", lambda e: e.tensor_tensor(out=stab[:, 24:28], in0=stab[:, 12:16], in1=stab[:, 16:20], op=ALU.subtract), r=["stab"], w=["stab"])
        T.op("act", lambda e: e.activation(out=stab[:, 20:28], in_=stab[:, 20:28], func=AF.Exp), r=["stab"], w=["stab"])
        T.op("act", lambda e: e.activation(out=stab[:, 28:32], in_=stab[:, 16:20], func=AF.Exp, scale=-1.0), r=["stab"], w=["stab"])
        T.op("dve", lambda e: e.tensor_scalar(out=stab[:, 40:44], in0=stab[:, 20:24], scalar1=1.0 / 16.0, scalar2=None, op0=ALU.mult), r=["stab"], w=["stab"])
        T.dma("sp", "o_sm", sm_d[:, :], stab[:, 16:20], r=["stab"])
        dd = junk16[:, 0:64].rearrange("p (s h) -> p s h", s=16)
        T.op("dve", lambda e: e.tensor_tensor(out=dd, in0=diag16[:, :].unsqueeze(2).to_broadcast([16, 16, 4]), in1=stab[:, 24:28].unsqueeze(1).to_broadcast([16, 16, 4]), op=ALU.mult),
             r=["diag16", "stab"], w=["junk16"])
        i, ps = PS()
        T.op("pe", lambda e: e.matmul(ps[:, 0:64], ones16[:, :], junk16[:, 0:64], start=True, stop=True), r=["ones16", "junk16"], w=[("ps", i)])
        T.op("act", lambda e: e.activation(out=decbc[:, :], in_=ps[:, 0:64], func=AF.Copy), r=[("ps", i)], w=["decbc"])

    slot_rot = [0]

    def make_bg(gen):
        state = {"done": False}

        def bg(n):
            for _ in range(n):
                if state["done"]:
                    return
                try:
                    next(gen)
                except StopIteration:
                    state["done"] = True

        def drain():
            while not state["done"]:
                bg(1)
        bg.drain = drain
        return bg

    def mlstm_samples(hp):
        for hl, h in enumerate((2 * hp, 2 * hp + 1)):
            i, ps = PS()
            psb = ps[:, :].bitcast(BF16)
            for dh in range(2):
                T.op("pe", lambda e, dh=dh: e.transpose(psb[0:16, dh * 128:(dh + 1) * 128], stq3[:, hl * 2 + dh, :], ident_bf[:]), r=["stq", "ident_bf"], w=[("ps", i)])
                T.op("pe", lambda e, dh=dh: e.transpose(psb[0:16, 256 + dh * 128:256 + (dh + 1) * 128], stk3[:, hl * 2 + dh, :], ident_bf[:]), r=["stk", "ident_bf"], w=[("ps", i)])
            T.op("act", lambda e: e.activation(out=qk_s[:, :], in_=psb[0:16, 0:512], func=AF.Copy), r=[("ps", i)], w=["qk_s"])
            T.op("dve", lambda e: e.scalar_tensor_tensor(out=junk16[:, :], in0=qk_s[:, 0:256], scalar=1.0 / 16.0, in1=qk_s[:, 256:512], op0=ALU.mult, op1=ALU.mult,
                                                         accum_out=stab[:, 32 + h:33 + h]), r=["qk_s", "junk16"], w=["junk16", "stab_q"])
            T.op("dve", lambda e: e.tensor_tensor(out=stab[:, 36 + h:37 + h], in0=stab[:, 32 + h:33 + h], in1=stab[:, 20 + h:21 + h], op=ALU.mult), r=["stab_q", "stab"], w=["stab_q"])
            qm3 = qm_s[:, :].rearrange("p (d s j) -> p d s j", d=2, s=16)
            for dh in range(2):
                T.op("dve", lambda e, dh=dh: e.tensor_tensor(out=qm3[:, dh], in0=stq3[:, hl * 2 + dh, :].unsqueeze(1).to_broadcast([128, 16, 16]),
                                                             in1=diagbc[:, :].rearrange("p (s j) -> p s j", s=16), op=ALU.mult), r=["stq", "diagbc"], w=["qm_s"])
            T.op("dve", lambda e: e.tensor_scalar(out=Hs[:, 0:256], in0=stv[0:16, hl * 256:(hl + 1) * 256], scalar1=stab[:, 40 + h:41 + h], scalar2=None, op0=ALU.mult),
                 r=["stv", "stab"], w=["Hs"])
            T.op("dve", lambda e: e.tensor_copy(out=Hs[:, 256:257], in_=stab[:, 40 + h:41 + h]), r=["stab", "Hs"], w=["Hs"])
            its = [(s_, dh) for s_ in range(16) for dh in range(2)]
            base = slot_rot[0]
            slot_rot[0] += len(its)

            def load(j):
                s_, dh = its[j]
                sl = (base + j) % NSL
                T.dma("sp", f"ci{sl}", cin[sl][:, :], caug_in_d[s_, h, dh * 128:(dh + 1) * 128, :], w=[f"cin{sl}"])
            for j in range(min(PFD, len(its))):
                load(j)
            i_q, ps_q = PS(hold=True)
            for j, (s_, dh) in enumerate(its):
                sl = (base + j) % NSL
                if j + PFD < len(its):
                    load(j + PFD)
                vm = vm_s[:, (s_ % 4) * 257:(s_ % 4 + 1) * 257]; vk = f"vm{s_ % 4}"
                if dh == 0:
                    T.op("pool", lambda e: e.tensor_scalar(out=vm, in0=Hs[:, 0:257], scalar1=diag16[:, s_:s_ + 1], scalar2=0.0, op0=ALU.mult, op1=ALU.add), r=["Hs", "diag16"], w=[vk])
                T.op("act", lambda e: e.activation(out=cinb[sl][:, :], in_=cin[sl][:, :], func=AF.Copy), r=[f"cin{sl}"], w=[f"cinb{sl}"])
                T.op("pe", lambda e: e.matmul(ps_q[0:16, 0:257], qm3[:, dh, s_, :], cinb[sl][:, :], start=(j == 0), stop=(j == len(its) - 1)),
                     r=["qm_s", f"cinb{sl}"], w=[("ps", i_q)])
                i_u, ps_u = PS()
                T.op("pe", lambda e: e.matmul(ps_u[:, 0:257], qk_s[0:16, 256 + dh * 128:256 + (dh + 1) * 128], vm, start=True, stop=True), r=["qk_s", vk], w=[("ps", i_u)])
                T.op("dve", lambda e: e.scalar_tensor_tensor(out=cout[sl][:, :], in0=cin[sl][:, :], scalar=decbc[:, s_ * 4 + h:s_ * 4 + h + 1], in1=ps_u[:, 0:257], op0=ALU.mult, op1=ALU.add),
                     r=[f"cin{sl}", "decbc", ("ps", i_u)], w=[f"cout{sl}"])
                T.dma("sp", f"co{sl}", scaug_d[s_, h, dh * 128:(dh + 1) * 128, :], cout[sl][:, :], r=[f"cout{sl}"])
                yield
            ps_held.discard(i_q)
            T.op("dve", lambda e: e.tensor_scalar(out=Hs[:, 0:257], in0=ps_q[0:16, 0:257], scalar1=stab[:, 24 + h:25 + h], scalar2=None, op0=ALU.mult),
                 r=[("ps", i_q), "stab"] + [f"vm{j}" for j in range(4)], w=["Hs"])
            T.op("dve", lambda e: e.scalar_tensor_tensor(out=Hs[:, 0:256], in0=stv[0:16, hl * 256:(hl + 1) * 256], scalar=stab[:, 36 + h:37 + h], in1=Hs[:, 0:256], op0=ALU.mult, op1=ALU.add),
                 r=["stv", "stab_q", "Hs"], w=["Hs"])
            T.op("dve", lambda e: e.tensor_tensor(out=Hs[:, 256:257], in0=Hs[:, 256:257], in1=stab[:, 36 + h:37 + h], op=ALU.add), r=["stab_q", "Hs"], w=["Hs"])
            post_mlstm(Hs[:, 0:257], ["Hs"], stab[:, 28 + h:29 + h], ["stab"], 16, 2, lambda k, h=h: to_Y(k, 16, h * 2, NP))
            yield

    def gla_samples(hp):
        for hl, h in enumerate((2 * hp, 2 * hp + 1)):
            i, ps = PS()
            psb = ps[:, :].bitcast(BF16)
            T.op("pe", lambda e: e.transpose(psb[0:16, 0:128], stq3[:, hl, :], ident_bf[:]), r=["stq", "ident_bf"], w=[("ps", i)])
            T.op("pe", lambda e: e.transpose(psb[0:16, 128:256], stk3[:, hl, :], ident_bf[:]), r=["stk", "ident_bf"], w=[("ps", i)])
            T.op("pe", lambda e: e.transpose(psb[0:16, 256:384], stk3[:, 2 + hl, :], ident_bf[:]), r=["stk", "ident_bf"], w=[("ps", i)])
            T.op("act", lambda e: e.activation(out=qk_s[:, 0:384], in_=psb[0:16, 0:384], func=AF.Copy), r=[("ps", i)], w=["qk_s"])
            T.op("dve", lambda e: e.scalar_tensor_tensor(out=junk16[:, 0:128], in0=qk_s[:, 0:128], scalar=1.0, in1=qk_s[:, 128:256], op0=ALU.mult, op1=ALU.mult,
                                                         accum_out=stab[:, 44 + h:45 + h]), r=["qk_s", "junk16"], w=["junk16", "stab_q"])
            qm3 = qm_s[:, 0:256].rearrange("p (s j) -> p s j", s=16)
            T.op("dve", lambda e: e.tensor_tensor(out=qm3, in0=stq3[:, hl, :].unsqueeze(1).to_broadcast([128, 16, 16]),
                                                  in1=diagbc[:, :].rearrange("p (s j) -> p s j", s=16), op=ALU.mult), r=["stq", "diagbc"], w=["qm_s"])
            base = slot_rot[0]
            slot_rot[0] += 16

            def load(j):
                sl = (base + j) % NSL
                T.dma("sp", f"ci{sl}", cin[sl][:, 0:256], s_in_d[j, h, :, :], w=[f"cin{sl}"])
            for j in range(PFD):
                load(j)
            i_q, ps_q = PS(hold=True)
            for s_ in range(16):
                sl = (base + s_) % NSL
                if s_ + PFD < 16:
                    load(s_ + PFD)
                vm = vm_s[:, (s_ % 4) * 257:(s_ % 4) * 257 + 256]; vk = f"vm{s_ % 4}"
                T.op("pool", lambda e: e.tensor_scalar(out=vm, in0=stv[0:16, hl * 256:(hl + 1) * 256], scalar1=diag16[:, s_:s_ + 1], scalar2=0.0, op0=ALU.mult, op1=ALU.add), r=["stv", "diag16"], w=[vk])
                T.op("act", lambda e: e.activation(out=cinb[sl][:, 0:256], in_=cin[sl][:, 0:256], func=AF.Copy), r=[f"cin{sl}"], w=[f"cinb{sl}"])
                T.op("pe", lambda e: e.matmul(ps_q[0:16, 0:256], qm3[:, s_, :], cinb[sl][:, 0:256], start=(s_ == 0), stop=(s_ == 15)), r=["qm_s", f"cinb{sl}"], w=[("ps", i_q)])
                i_u, ps_u = PS()
                T.op("pe", lambda e: e.matmul(ps_u[:, 0:256], qk_s[0:16, 256:384], vm, start=True, stop=True), r=["qk_s", vk], w=[("ps", i_u)])
                T.op("dve", lambda e: e.scalar_tensor_tensor(out=cout[sl][:, 0:256], in0=cin[sl][:, 0:256], scalar=elast_s[:, h * 16 + s_:h * 16 + s_ + 1], in1=ps_u[:, 0:256], op0=ALU.mult, op1=ALU.add),
                     r=[f"cin{sl}", "elast_s", ("ps", i_u)], w=[f"cout{sl}"])
                T.dma("sp", f"co{sl}", ss_d[s_, h, :, :], cout[sl][:, 0:256], r=[f"cout{sl}"])
                yield
            ps_held.discard(i_q)
            T.op("dve", lambda e: e.scalar_tensor_tensor(out=Hs[:, 0:256], in0=stv[0:16, hl * 256:(hl + 1) * 256], scalar=stab[:, 44 + h:45 + h], in1=ps_q[0:16, 0:256], op0=ALU.mult, op1=ALU.add),
                 r=["stv", "stab_q", ("ps", i_q)], w=["Hs"])
            post_gla(Hs[:, 0:256], ["Hs"], 16, 2, lambda k, h=h: to_Y(k, 16, 8 + h * 2, NP))
            yield

    wprefetch(C_K + 0 * 512)
    tok3 = mlstm_gates(xTp3, XPK, False, X)
    GK = ["g_itil", "g_logf", "g_b", "g_g", "g_st", "g_big", "tokT"]
    for q4 in range(4):
        T.dma("pool", f"xm{q4}", X3[:, 4 * q4:4 * q4 + 4, :], xT_r[:, 4 * q4:4 * q4 + 4, :], r=["tokT"], w=(["X"] if q4 == 3 else [f"X_{q4}"]) + (GK[:-1] if q4 == 0 else []))
    mlstm_segment(0, False, tok3, nxt=lambda: wprefetch(C_K + 512))
    mlstm_segment(1, False, tok3, nxt=lambda: wprefetch(C_BK, 256))
    T.barrier()
    gla_bg(xTp3, XPK, TB_PRE)
    gla_segment(0, False, nxt=lambda: wprefetch(C_BK + 256, 256))
    gla_segment(1, False)
    T.barrier()
    CK = [f"Caug{h}" for h in range(4)]; SK = [f"Sst{h}" for h in range(4)]
    T.op("dve", lambda e: e.tensor_scalar(out=Caug[:], in0=Caug[:], scalar1=flag[:, 0:1], scalar2=None, op0=ALU.mult), r=["flag"] + CK, w=CK)
    T.op("act", lambda e: e.activation(out=Cbf[:], in_=Caug[:], func=AF.Copy), r=CK, w=[f"Cbf{h}" for h in range(4)])
    T.op("dve", lambda e: e.tensor_scalar(out=Sst[:], in0=Sst[:], scalar1=flag[:, 0:1], scalar2=None, op0=ALU.mult), r=["flag"] + SK, w=SK)
    T.op("act", lambda e: e.activation(out=Sbf[:], in_=Sst[:], func=AF.Copy), r=SK, w=[f"Sbf{h}" for h in range(4)])
    T.op("dve", lambda e: e.tensor_scalar(out=mstate[:], in0=mstate[:], scalar1=flag[0:4, 0:1], scalar2=None, op0=ALU.mult), r=["flag", "mstate"], w=["mstate"])
    T.barrier()
    wviews[:] = WV2
    wslot[0] = 0
    wprefetch(C_Q)
    tok3 = mlstm_gates(X3, XK, True, Y)
    sample_gates()
    DEFER = False
    if not DEFER:
        def seg_m(hp, pref):
            def f():
                stash_mlstm()
                pref()
            return f
        bg = make_bg(mlstm_samples(0))
        mlstm_segment(0, True, tok3, bg=bg, nxt=seg_m(0, lambda: wprefetch(C_Q + 512)))
        bg.drain()
        bg = make_bg(mlstm_samples(1))
        mlstm_segment(1, True, tok3, bg=bg, nxt=seg_m(1, lambda: wprefetch(C_BQ, 256)))
        bg.drain()
        T.barrier()
        gla_bg(X3, XK, TB_MAIN)
        bg = make_bg(gla_samples(0))
        gla_segment(0, True, bg=bg, nxt=lambda: (stash_gla(), wprefetch(C_BQ + 256, 256)))
        bg.drain()
        bg = make_bg(gla_samples(1))
        gla_segment(1, True, bg=bg, nxt=lambda: (stash_gla(),))
        bg.drain()
        bgB1 = make_bg(iter(()))
        T.barrier()
    else:
        def nx(pref, stash):
            def f():
                stash()
                pref()
            return f
        mlstm_segment(0, True, tok3, nxt=nx(lambda: wprefetch(C_Q + 512), stash_mlstm))
        bgA0 = make_bg(mlstm_samples(0))
        mlstm_segment(1, True, tok3, bg2=bgA0, bgn=2, nxt=lambda: (bgA0.drain(), stash_mlstm(), wprefetch(C_BQ, 256)))
        T.barrier()
        gla_bg(X3, XK, TB_MAIN)
        bgA1 = make_bg(mlstm_samples(1))
        gla_segment(0, True, bg2=bgA1, bgn=4, nxt=lambda: (bgA1.drain(), stash_gla(), wprefetch(C_BQ + 256, 256)))
        bgB0 = make_bg(gla_samples(0))
        gla_segment(1, True, bg2=bgB0, bgn=2, nxt=lambda: (bgB0.drain(), stash_gla()))
        bgB1 = make_bg(gla_samples(1))
        T.barrier()
    wviews[:] = WV4
    wslot[0] = 0
    for h in range(4):
        for dh in range(2):
            sl = slice((h * 2 + dh) * 257, (h * 2 + dh + 1) * 257)
            T.dma("sp", f"o_pc{h}{dh}", pcaug_d[h, dh * 128:(dh + 1) * 128, :], Caug[:, sl], r=[f"Caug{h}"])
        T.dma("sp", f"o_ps{h}", ps_d[h, :, :], Sst[:, h * 256:(h + 1) * 256], r=[f"Sst{h}"])
    T.dma("sp", "o_pm", pm_d[:, :], mstate[:, :], r=["mstate"])
    T.dma("sp", "o_pconv", pconv_d[:, :], pconv_sb[:, :], r=["pconv_sb"])
    T.dma("sp", "o_sconv", sconv_d[:, :], sconv_sb[:, :], r=["sconv_sb"])

    for g in range(2):
        Wo, Wok = wload(C_O + g * 512)
        Wz, Wzk = wload(C_Z + g * 512)
        for cbi in range(4):
            blk = g * 4 + cbi
            for (t0, n) in TB_MAIN:
                io, pso = fm_proj(Wo, Wok, cbi, X3, XK, t0, n)
                iz, psz = fm_proj(Wz, Wzk, cbi, X3, XK, t0, n)
                T.op("act", lambda e: e.activation(out=accb[0][:, 0:n], in_=pso[:, 0:n], func=AF.Sigmoid), r=[("ps", io)], w=["acc0"])
                T.op("act", lambda e: e.activation(out=accb[1][:, 0:n], in_=psz[:, 0:n], func=AF.Sigmoid), r=[("ps", iz)], w=["acc1"])
                T.op("dve", lambda e: e.tensor_tensor(out=accb[0][:, 0:n], in0=accb[0][:, 0:n], in1=accb[1][:, 0:n], op=ALU.mult), r=["acc0", "acc1"], w=["acc0"])
                T.op("dve", lambda e: e.tensor_tensor(out=accb[0][:, 0:n], in0=accb[0][:, 0:n], in1=psz[:, 0:n], op=ALU.mult), r=["acc0", ("ps", iz)], w=["acc0"])
                T.op("dve", lambda e: e.scalar_tensor_tensor(out=Y3[:, blk, t0:t0 + n], in0=Y3[:, blk, t0:t0 + n], scalar=gna[:, blk:blk + 1], in1=accb[0][:, 0:n], op0=ALU.mult, op1=ALU.mult),
                     r=["Y", "gna", "acc0"], w=["Y"])
                bgB1(2)
    bgB1.drain()
    for g in range(2):
        Wz, Wzk = wload(C_BZ + g * 512)
        for cbi in range(4):
            blk = g * 4 + cbi
            for (t0, n) in TB_MAIN:
                iz, psz = fm_proj(Wz, Wzk, cbi, X3, XK, t0, n)
                T.op("act", lambda e: e.activation(out=accb[1][:, 0:n], in_=psz[:, 0:n], func=AF.Sigmoid), r=[("ps", iz)], w=["acc1"])
                T.op("dve", lambda e: e.tensor_tensor(out=accb[1][:, 0:n], in0=accb[1][:, 0:n], in1=psz[:, 0:n], op=ALU.mult), r=["acc1", ("ps", iz)], w=["acc1"])
                T.op("dve", lambda e: e.scalar_tensor_tensor(out=Y3[:, 8 + blk, t0:t0 + n], in0=Y3[:, 8 + blk, t0:t0 + n], scalar=gnb[:, blk:blk + 1], in1=accb[1][:, 0:n], op0=ALU.mult, op1=ALU.mult),
                     r=["Y", "gnb", "acc1"], w=["Y"])
    T.barrier()

    M3 = QR[:, 0:16 * NM].rearrange("p (a b) -> p a b", a=16)
    w_pa_r = w_pa_d.rearrange("(kc p) n -> p kc n", p=128)
    w_pb_r = w_pb_d.rearrange("(kc p) n -> p kc n", p=128)
    for fbg in range(8):
        s2 = fbg % 2
        base = s2 * 12288
        Wpa3 = A[:, base:base + 2048].rearrange("p (a b) -> p a b", a=8)
        Wpb3 = A[:, base + 2048:base + 4096].rearrange("p (a b) -> p a b", a=8)
        Wga3 = A[:, base + 4096:base + 8192].rearrange("p (a b) -> p a b", a=16)
        Wgb3 = A[:, base + 8192:base + 12288].rearrange("p (a b) -> p a b", a=16)
        c0 = fbg * 256
        T.dma("pool", f"v{s2}a", Wpa3, w_pa_r[:, :, c0:c0 + 256], w=[f"V{s2}a"])
        T.dma("pool", f"v{s2}b", Wpb3, w_pb_r[:, :, c0:c0 + 256], w=[f"V{s2}b"])
        T.dma("pool", f"v{s2}c", Wga3, w_in_r[:, :, C_GA + c0:C_GA + c0 + 256], w=[f"V{s2}c"])
        T.dma("pool", f"v{s2}d", Wgb3, w_in_r[:, :, C_GB + c0:C_GB + c0 + 256], w=[f"V{s2}d"])
        for fbl in range(2):
            fb = fbg * 2 + fbl
            for (t0, n) in TB_MAIN:
                ia, psa = PS()
                for kc in range(8):
                    T.op("pe", lambda e, kc=kc: e.matmul(psa[:, 0:n], Wpa3[:, kc, fbl * 128:(fbl + 1) * 128], Y3[:, kc, t0:t0 + n], start=(kc == 0), stop=(kc == 7)),
                         r=[f"V{s2}a", "Y"], w=[("ps", ia)])
                ib, psb_ = PS()
                for kc in range(8):
                    T.op("pe", lambda e, kc=kc: e.matmul(psb_[:, 0:n], Wpb3[:, kc, fbl * 128:(fbl + 1) * 128], Y3[:, 8 + kc, t0:t0 + n], start=(kc == 0), stop=(kc == 7)),
                         r=[f"V{s2}b", "Y"], w=[("ps", ib)])
                iga, psga = fm_proj(Wga3, [f"V{s2}c"], fbl, X3, XK, t0, n)
                igb, psgb = fm_proj(Wgb3, [f"V{s2}d"], fbl, X3, XK, t0, n)
                T.op("act", lambda e: e.activation(out=accb[0][:, 0:n], in_=psga[:, 0:n], func=AF.Sigmoid), r=[("ps", iga)], w=["acc0"])
                T.op("act", lambda e: e.activation(out=accb[1][:, 0:n], in_=psgb[:, 0:n], func=AF.Sigmoid), r=[("ps", igb)], w=["acc1"])
                T.op("dve", lambda e: e.tensor_tensor(out=accb[0][:, 0:n], in0=accb[0][:, 0:n], in1=psa[:, 0:n], op=ALU.mult), r=["acc0", ("ps", ia)], w=["acc0"])
                T.op("dve", lambda e: e.tensor_tensor(out=accb[1][:, 0:n], in0=accb[1][:, 0:n], in1=psb_[:, 0:n], op=ALU.mult), r=["acc1", ("ps", ib)], w=["acc1"])
                T.op("dve", lambda e: e.tensor_tensor(out=M3[:, fb, t0:t0 + n], in0=accb[0][:, 0:n], in1=accb[1][:, 0:n], op=ALU.add), r=["acc0", "acc1"], w=["M"])
    T.barrier()

    wout3 = XA[:, 0:16 * 2048].rearrange("p (a b) -> p a b", a=16)
    w_out_r = w_out_d.rearrange("(kc p) n -> p kc n", p=128)
    for q4 in range(4):
        T.dma("pool", f"wo{q4}", wout3[:, q4 * 4:(q4 + 1) * 4, :], w_out_r[:, q4 * 4:(q4 + 1) * 4, :], w=[f"wout{q4}"])
    WOK = [f"wout{q4}" for q4 in range(4)]
    lnbase = 16 * 2048
    lng = XA[:, lnbase:lnbase + 4096].bitcast(F32)
    lnb = XA[:, lnbase + 4096:lnbase + 8192].bitcast(F32)
    T.dma("sp", "c_lng", lng, lng_d[:, :], w=["lng"])
    T.dma("sp", "c_lnb", lnb, lnb_d[:, :], w=["lnb"])
    zt = [Y[:, k * 4096:(k + 1) * 4096].bitcast(F32) for k in range(2)]
    xt_ = [Y[:, 8192 + k * 4096:8192 + (k + 1) * 4096].bitcast(F32) for k in range(2)]
    def xt_load(tl):
        m = 128 if tl < 8 else 16
        k = tl % 2
        T.dma("sp", f"xt{k}", xt_[k][0:m, :], xtok_d[tl * 128:tl * 128 + m, :], w=[f"xt{k}"])
    xt_load(0)
    for tl in range(9):
        m = 128 if tl < 8 else 16
        k = tl % 2
        if tl + 1 < 9:
            xt_load(tl + 1)
        sm = small[k]; sk = f"small{k}"
        for g in range(4):
            i, ps = PS()
            for kc in range(16):
                T.op("pe", lambda e, kc=kc: e.matmul(ps[0:m, 0:512], M3[:, kc, tl * 128:tl * 128 + m], wout3[:, kc, g * 512:(g + 1) * 512], start=(kc == 0), stop=(kc == 15)),
                     r=["M", f"wout{kc // 4}"], w=[("ps", i)])
            T.op("dve", lambda e: e.scalar_tensor_tensor(out=zt[k][0:m, g * 512:(g + 1) * 512], in0=xt_[k][0:m, g * 512:(g + 1) * 512], scalar=float(ALPHA), in1=ps[0:m, 0:512], op0=ALU.mult, op1=ALU.add),
                 r=[f"xt{k}", ("ps", i)], w=[f"zt{k}"])
            T.op("dve", lambda e: e.bn_stats(out=sm[0:m, 8 + g * 6:14 + g * 6], in_=zt[k][0:m, g * 512:(g + 1) * 512]), r=[f"zt{k}", sk], w=[sk])
        T.op("dve", lambda e: e.bn_aggr(out=sm[0:m, 2:4], in_=sm[0:m, 8:32]), r=[sk], w=[sk])
        T.op("dve", lambda e: e.tensor_scalar(out=sm[0:m, 1:2], in0=sm[0:m, 3:4], scalar1=EPS, scalar2=None, op0=ALU.add), r=[sk], w=[sk])
        T.op("act", lambda e: e.activation(out=sm[0:m, 4:5], in_=sm[0:m, 1:2], func=AF.Ln), r=[sk], w=[sk])
        T.op("act", lambda e: e.activation(out=sm[0:m, 5:6], in_=sm[0:m, 4:5], func=AF.Exp, scale=-0.5), r=[sk], w=[sk])
        T.op("dve", lambda e: e.tensor_scalar(out=zt[k][0:m, :], in0=zt[k][0:m, :], scalar1=sm[0:m, 2:3], scalar2=sm[0:m, 5:6], op0=ALU.subtract, op1=ALU.mult), r=[f"zt{k}", sk], w=[f"zt{k}"])
        T.op("pool", lambda e: e.tensor_tensor(out=zt[k][0:m, :], in0=zt[k][0:m, :], in1=lng[0:m, :], op=ALU.mult), r=[f"zt{k}", "lng"], w=[f"zt{k}"])
        T.op("pool", lambda e: e.tensor_tensor(out=zt[k][0:m, :], in0=zt[k][0:m, :], in1=lnb[0:m, :], op=ALU.add), r=[f"zt{k}", "lnb"], w=[f"zt{k}"])
        T.dma("sp", f"oy{k}", y_d[tl * 128:tl * 128 + m, :], zt[k][0:m, :], r=[f"zt{k}"])
    T.finish()
    return nc


_CACHE = {}


def _host_inputs(inp):
    f = np.float32
    x_prompt = np.asarray(inp["x_prompt"], f); x_sample = np.asarray(inp["x_sample"], f)
    C = np.asarray(inp["state_mlstm_C"], f)[0]; n = np.asarray(inp["state_mlstm_n"], f)[0]
    m = np.asarray(inp["state_mlstm_m"], f)[0]; cv = np.asarray(inp["state_conv"], f)[0]
    S = np.asarray(inp["state_gla_S"], f)[0]
    caug = np.concatenate([C, n[..., None]], axis=-1)
    w_in = np.ascontiguousarray(np.asarray(inp["w_in"], f)[0])
    w_pa = np.ascontiguousarray(np.asarray(inp["w_pa"], f)[0]); w_pb = np.ascontiguousarray(np.asarray(inp["w_pb"], f)[0])
    w_out = np.ascontiguousarray(np.asarray(inp["w_out"], f)[0]); w_gu = np.ascontiguousarray(np.asarray(inp["w_gate_up"], f)[0])
    conv_w = np.asarray(inp["conv_w"], f)[0]; conv_b = np.asarray(inp["conv_b"], f)[0]
    cw = np.ascontiguousarray(conv_w.T.reshape(16, 128, 4).transpose(1, 0, 2).reshape(128, 64))
    cb = np.ascontiguousarray(conv_b.reshape(16, 128).T)
    bgate = np.ascontiguousarray(np.asarray(inp["b_gate"], f)[0].reshape(4, 128).T)
    b_i = np.asarray(inp["b_i"], f)[0]; b_f = np.asarray(inp["b_f"], f)[0]
    gna = np.ascontiguousarray(np.asarray(inp["a_norm_g"], f)[0].reshape(8, 128).T)
    gnb = np.ascontiguousarray(np.asarray(inp["b_norm_g"], f)[0].reshape(8, 128).T)
    lng = np.ascontiguousarray(np.broadcast_to(np.asarray(inp["ln_g"], f)[0][None, :], (128, D)))
    lnb = np.ascontiguousarray(np.broadcast_to(np.asarray(inp["ln_b"], f)[0][None, :], (128, D)))
    ident = np.eye(128, dtype=f)
    tri = (np.arange(128)[:, None] <= np.arange(128)[None, :]).astype(f)
    rmask = np.ones((128, NM), f); rmask[:, 0:NP:128] = 0.0; rmask[:, NP:] = 0.0
    shared = dict(w_in=w_in, w_pa=w_pa, w_pb=w_pb, w_out=w_out, w_gu=w_gu, cw=cw, cb=cb, bgate=bgate,
                  bi=b_i.reshape(4, 1).copy(), bf=b_f.reshape(4, 1).copy(),
                  bi_row=np.ascontiguousarray(np.broadcast_to(b_i[None, :], (16, 4))),
                  bf_row=np.ascontiguousarray(np.broadcast_to(b_f[None, :], (16, 4))),
                  gna=gna, gnb=gnb, lng=lng, lnb=lnb, ident=ident, maskA=tri * f(1.0 / 16.0), maskB=tri.copy(),
                  rmask=rmask, diag16=np.eye(16, dtype=f),
                  diagbc=np.ascontiguousarray(np.broadcast_to(np.eye(16, dtype=f).reshape(1, 256), (128, 256))))
    maps = []
    for c in range(8):
        b, th = c // 2, c % 2
        xm = np.concatenate([x_prompt[b, th * NP:(th + 1) * NP], x_sample[c * NS:(c + 1) * NS, 0]], axis=0)
        d = dict(shared)
        d["xT"] = np.ascontiguousarray(xm.T); d["xtok"] = np.ascontiguousarray(xm)
        d["xTp"] = np.ascontiguousarray(x_prompt[b, 0:NP].T)
        d["flag"] = np.full((128, 1), float(th), f)
        d["caug_in"] = np.ascontiguousarray(caug[c * NS:(c + 1) * NS]); d["s_in"] = np.ascontiguousarray(S[c * NS:(c + 1) * NS])
        d["m_in"] = np.ascontiguousarray(m[c * NS:(c + 1) * NS])
        cvs = cv[c * NS:(c + 1) * NS]
        d["conv_in"] = np.ascontiguousarray(cvs.transpose(2, 1, 0).reshape(16, 128, 3, 16).transpose(1, 0, 2, 3).reshape(128, 16 * 3 * 16))
        maps.append(d)
    return maps


def kernel(**inputs):
    if "nc" not in _CACHE:
        _CACHE["nc"] = build_nc()
    nc = _CACHE["nc"]
    maps = _host_inputs(inputs)
    res = run_bass_kernel_spmd(nc, maps, core_ids=list(range(8)))
    R = res.results
    f = np.float32
    y_prompt = np.zeros((4, 2048, D), f); y_sample = np.zeros((128, 1, D), f)
    pC = np.zeros((1, 4, 4, 256, 256), f); pn = np.zeros((1, 4, 4, 256), f); pm = np.zeros((1, 4, 4), f)
    pconv = np.zeros((1, 4, 3, 2048), f); pS = np.zeros((1, 4, 4, 128, 256), f)
    sC = np.zeros((1, 128, 4, 256, 256), f); sn = np.zeros((1, 128, 4, 256), f); sm = np.zeros((1, 128, 4), f)
    sconv = np.zeros((1, 128, 3, 2048), f); sS = np.zeros((1, 128, 4, 128, 256), f)
    for c in range(8):
        b, th = c // 2, c % 2
        r = R[c]
        y = np.asarray(r["y"])
        y_prompt[b, th * NP:(th + 1) * NP] = y[0:NP]
        y_sample[c * NS:(c + 1) * NS, 0] = y[NP:NM]
        if th == 1:
            ca = np.asarray(r["pcaug"])
            pC[0, b] = ca[:, :, 0:256]; pn[0, b] = ca[:, :, 256]
            pm[0, b] = np.asarray(r["pm"])[:, 0]
            pS[0, b] = np.asarray(r["ps"])
            pconv[0, b] = np.asarray(r["pconv"]).reshape(128, 16, 3).transpose(2, 1, 0).reshape(3, 2048)
        sa = np.asarray(r["scaug"])
        sC[0, c * NS:(c + 1) * NS] = sa[..., 0:256]; sn[0, c * NS:(c + 1) * NS] = sa[..., 256]
        sm[0, c * NS:(c + 1) * NS] = np.asarray(r["sm"])
        sS[0, c * NS:(c + 1) * NS] = np.asarray(r["ss"])
        sconv[0, c * NS:(c + 1) * NS] = np.asarray(r["sconv"]).reshape(128, 16, 3, 16).transpose(3, 2, 1, 0).reshape(16, 3, 2048)
    return (y_prompt, y_sample, pC, pn, pm, pconv, pS, sC, sn, sm, sconv, sS)
```

```python
import numpy as np
import concourse.bass as bass
import concourse.mybir as mybir
from concourse.bass_utils import run_bass_kernel_spmd

F32 = mybir.dt.float32
BF16 = mybir.dt.bfloat16
AF = mybir.ActivationFunctionType
ALU = mybir.AluOpType
AX = mybir.AxisListType

D = 2048
NP = 1024
NS = 16
NM = NP + NS
NCH = NP // 128
N_IN = 12312
ALPHA = 2.0 ** 0.25
EPS = 1e-5
SAME_SYNC = ('act', 'dve', 'pool', 'sp')

C_Q, C_K, C_V, C_I, C_F, C_O, C_Z = 0, 1024, 2048, 3072, 3076, 3080, 4104
C_BQ, C_BK, C_BV, C_BG, C_BZ, C_GA, C_GB = 5128, 5640, 6152, 7176, 7192, 8216, 10264


class Trk:
    def __init__(self, nc):
        self.nc = nc
        self.eng = dict(pe=nc.tensor, act=nc.scalar, dve=nc.vector, pool=nc.gpsimd, sp=nc.sync)
        self.sem = {k: nc.alloc_semaphore("sem_" + k) for k in self.eng}
        self.cnt = {k: 0 for k in self.eng}
        self.seen = {k: {} for k in self.eng}
        self.bufs = {}
        self.streams = {}
        self.ps_next = 0

    def _wait(self, eng, tok):
        key, sem, val = tok
        if key == eng and eng not in SAME_SYNC:
            return
        if self.seen[eng].get(key, 0) >= val:
            return
        self.eng[eng].wait_ge(sem, val)
        self.seen[eng][key] = val

    def _deps(self, eng, r, w):
        for k in r:
            b = self.bufs.get(k)
            if b and b[0]:
                self._wait(eng, b[0])
        for k in w:
            b = self.bufs.get(k)
            if b:
                if b[0]:
                    self._wait(eng, b[0])
                for t in b[1].values():
                    self._wait(eng, t)

    def _record(self, tok, r, w):
        for k in r:
            b = self.bufs.setdefault(k, [None, {}])
            b[1][tok[0]] = tok
        for k in w:
            self.bufs[k] = [tok, {}]

    def op(self, eng, fn, r=(), w=()):
        self._deps(eng, r, w)
        ins = fn(self.eng[eng])
        self.cnt[eng] += 1
        ins.then_inc(self.sem[eng], 1)
        self._record((eng, self.sem[eng], self.cnt[eng]), r, w)
        return ins

    def dma(self, eng, stream, out, in_, r=(), w=()):
        if stream not in self.streams:
            self.streams[stream] = [self.nc.alloc_semaphore("dsem_" + stream), 0]
        st = self.streams[stream]
        self._deps(eng, r, w)
        ins = self.eng[eng].dma_start(out=out, in_=in_)
        st[1] += 16
        ins.then_inc(st[0], 16)
        self._record(("dma:" + stream, st[0], st[1]), r, w)

    def barrier(self):
        names = list(self.eng)
        for e in names:
            for o in names:
                if o != e and self.cnt[o] > 0:
                    self._wait(e, (o, self.sem[o], self.cnt[o]))
            for s, (sem, val) in self.streams.items():
                if val > 0:
                    self._wait(e, ("dma:" + s, sem, val))

    def finish(self):
        for s, (sem, val) in self.streams.items():
            if val > 0:
                self._wait("sp", ("dma:" + s, sem, val))
        for o in self.eng:
            if o != "sp" and self.cnt[o] > 0:
                self._wait("sp", (o, self.sem[o], self.cnt[o]))


def build_nc():
    nc = bass.Bass("TRN2", target_bir_lowering=False)

    def din(name, shape):
        return nc.dram_tensor(name, list(shape), F32, kind="ExternalInput").ap()

    def dout(name, shape):
        return nc.dram_tensor(name, list(shape), F32, kind="ExternalOutput").ap()

    xT_d = din("xT", [D, NM]); xTp_d = din("xTp", [D, NP]); xtok_d = din("xtok", [NM, D])
    flag_d = din("flag", [128, 1])
    w_in_d = din("w_in", [D, N_IN]); w_pa_d = din("w_pa", [1024, D]); w_pb_d = din("w_pb", [1024, D])
    w_out_d = din("w_out", [D, D]); w_gu_d = din("w_gu", [16, 512])
    cw_d = din("cw", [128, 16 * 4]); cb_d = din("cb", [128, 16]); bgate_d = din("bgate", [128, 4])
    bi_d = din("bi", [4, 1]); bf_d = din("bf", [4, 1]); bi_row_d = din("bi_row", [16, 4]); bf_row_d = din("bf_row", [16, 4])
    gna_d = din("gna", [128, 8]); gnb_d = din("gnb", [128, 8])
    lng_d = din("lng", [128, D]); lnb_d = din("lnb", [128, D])
    ident_d = din("ident", [128, 128]); maskA_d = din("maskA", [128, 128]); maskB_d = din("maskB", [128, 128])
    rmask_d = din("rmask", [128, NM]); diag_d = din("diag16", [16, 16]); diagbc_d = din("diagbc", [128, 256])
    caug_in_d = din("caug_in", [16, 4, 256, 257]); s_in_d = din("s_in", [16, 4, 128, 256])
    m_in_d = din("m_in", [16, 4]); conv_in_d = din("conv_in", [128, 16 * 3 * 16])

    y_d = dout("y", [NM, D])
    pcaug_d = dout("pcaug", [4, 256, 257]); ps_d = dout("ps", [4, 128, 256]); pm_d = dout("pm", [4, 1])
    pconv_d = dout("pconv", [128, 16 * 3])
    scaug_d = dout("scaug", [16, 4, 256, 257]); ss_d = dout("ss", [16, 4, 128, 256]); sm_d = dout("sm", [16, 4])
    sconv_d = dout("sconv", [128, 16 * 3 * 16])

    T = Trk(nc)
    uid = [0]

    def sb(shape, dt, name=None):
        uid[0] += 1
        return nc.alloc_sbuf_tensor(name or f"t{uid[0]}", list(shape), dt)

    ident_bf = sb([128, 128], BF16); ident_f = sb([128, 128], F32)
    maskA = sb([128, 128], F32); maskB = sb([128, 128], F32)
    rmask = sb([128, NM], BF16)
    diag16 = sb([16, 16], F32); ones16 = sb([16, 128], F32)
    cw = sb([128, 64], F32); cb = sb([128, 16], F32); bgate = sb([128, 4], F32)
    bi = sb([4, 1], F32); bff = sb([4, 1], F32); bi_row = sb([16, 4], F32); bf_row = sb([16, 4], F32)
    gna = sb([128, 8], F32); gnb = sb([128, 8], F32); flag = sb([128, 1], F32)
    wgu = sb([16, 512], BF16); wif = sb([128, 16 * 8], BF16); wbg = sb([128, 16 * 16], BF16)
    Caug = sb([128, 8 * 257], F32); Cbf = sb([128, 8 * 257], BF16)
    Sst = sb([128, 4 * 256], F32); Sbf = sb([128, 4 * 256], BF16)
    tokT = sb([128, NCH * 100], F32)
    histq = sb([128, 16 * 3], F32)
    pconv_sb = sb([128, 16 * 3], F32)
    conv_in = sb([128, 16 * 3 * 16], F32); sconv_sb = sb([128, 16 * 3 * 16], F32)
    xTp_tail = sb([128, 16 * 4], BF16)
    elast = sb([128, 4 * NCH], F32)
    mstate = sb([4, 1], F32)
    Sm = [sb([128, 128], BF16) for _ in range(2)]
    vt = [sb([128, 257], BF16) for _ in range(2)]
    vh = [sb([128, 257], BF16) for _ in range(2)]
    ktok = [sb([128, 256], BF16) for _ in range(2)]
    hnb = [sb([128, 256], BF16) for _ in range(2)] + [sb([16, 256], BF16)]
    small = [sb([128, 32], F32) for _ in range(2)] + [sb([16, 32], F32)]
    sacc_t = sb([128, 16], F32)
    stab = sb([16, 64], F32)
    qm_s = sb([128, 2 * 16 * 16], BF16)
    vm_s = sb([16, 4 * 257], BF16)
    accb = [sb([128, 512], F32) for _ in range(2)]
    decbc = sb([128, 64], F32)
    Hs = sb([16, 257], F32)
    NSL = 6
    PFD = 4
    bgT = sb([16, NM], BF16)
    stq = sb([128, 64], BF16); stk = sb([128, 64], BF16); stv = sb([16, 512], BF16)
    diagbc = sb([128, 256], BF16)
    qk_s = sb([16, 512], BF16)
    elast_s = sb([128, 4 * 16], F32)
    last16 = sb([128, 2 * NCH], F32)
    junk16 = sb([16, 256], F32)
    Y = sb([128, 16 * NM], BF16)
    QR = sb([128, 12928 + 4160], BF16)
    Q = QR[:, 0:12928]
    R = QR[:, 12928:12928 + 4160]
    remaining = nc.sbuf_bytes_remaining
    NWS = 3
    assert remaining >= 2 * (16 * NM + NWS * 8192), remaining
    XA = sb([128, 16 * NM + NWS * 8192], BF16)
    X = XA[:, 0:16 * NM]
    A = XA[:, 16 * NM:16 * NM + NWS * 8192]

    print('SBUF remaining bytes:', nc.sbuf_bytes_remaining)
    SLW = 1286
    cin = [A[:, 16384 + k * SLW:16384 + k * SLW + 514].bitcast(F32) for k in range(NSL)]
    cinb = [A[:, 16384 + k * SLW + 514:16384 + k * SLW + 771] for k in range(NSL)]
    cout = [A[:, 16384 + k * SLW + 772:16384 + k * SLW + 1286].bitcast(F32) for k in range(NSL)]
    psum = [nc.alloc_psum_tensor(f"ps{i}", [128, 512], F32) for i in range(8)]

    ps_held = set()

    def PS(hold=False):
        i = T.ps_next
        while i in ps_held:
            i = (i + 1) % 8
        T.ps_next = (i + 1) % 8
        if hold:
            ps_held.add(i)
        return i, psum[i]

    xTp = Y

    def v3(t, a, b):
        return t[:, 0:a * b].rearrange("p (a b) -> p a b", a=a)

    X3 = v3(X, 16, NM); Y3 = v3(Y, 16, NM); xTp3 = v3(xTp, 16, NP)

    def ld(dst, src, key, eng="sp"):
        T.dma(eng, "c_" + key, dst, src, w=[key])

    ld(ident_f[:], ident_d[:, :], "ident_f"); ld(ident_bf[:], ident_d[:, :], "ident_bf", "pool")
    ld(maskA[:], maskA_d[:, :], "maskA"); ld(maskB[:], maskB_d[:, :], "maskB")
    ld(rmask[:], rmask_d[:, :], "rmask", "pool"); ld(diag16[:], diag_d[:, :], "diag16")
    ld(cw[:], cw_d[:, :], "cw"); ld(cb[:], cb_d[:, :], "cb"); ld(bgate[:], bgate_d[:, :], "bgate")
    ld(bi[:], bi_d[:, :], "bi"); ld(bff[:], bf_d[:, :], "bf"); ld(bi_row[:], bi_row_d[:, :], "bi_row")
    ld(bf_row[:], bf_row_d[:, :], "bf_row"); ld(gna[:], gna_d[:, :], "gna"); ld(gnb[:], gnb_d[:, :], "gnb")
    ld(flag[:], flag_d[:, :], "flag"); ld(wgu[:], w_gu_d[:, :], "wgu", "pool")
    ld(conv_in[:], conv_in_d[:, :], "conv_in")
    ld(diagbc[:], diagbc_d[:, :], "diagbc", "pool")
    w_in_r = w_in_d.rearrange("(kc p) n -> p kc n", p=128)
    ld(v3(wif, 16, 8), w_in_r[:, :, C_I:C_I + 8], "wif", "pool")
    ld(v3(wbg, 16, 16), w_in_r[:, :, C_BG:C_BG + 16], "wbg", "pool")
    T.op("dve", lambda e: e.memset(ones16[:], 1.0), w=["ones16"])
    xTp_r = xTp_d.rearrange("(kc p) n -> p kc n", p=128)
    xT_r = xT_d.rearrange("(kc p) n -> p kc n", p=128)
    for q4 in range(4):
        T.dma("pool", f"xp{q4}", xTp3[:, 4 * q4:4 * q4 + 4, :], xTp_r[:, 4 * q4:4 * q4 + 4, :], w=["xTp"] if q4 == 3 else [f"xTp_{q4}"])
    XPK = ["xTp", "xTp_0", "xTp_1", "xTp_2"]
    XK = ["X", "X_0", "X_1", "X_2"]
    T.op("dve", lambda e: e.tensor_copy(out=v3(xTp_tail, 16, 4), in_=xTp3[:, :, NP - 4:NP]), r=XPK, w=["xTp_tail"])
    T.op("dve", lambda e: e.memset(Caug[:], 0.0), w=["Caug"])
    T.op("dve", lambda e: e.memset(Cbf[:], 0.0), w=["Cbf"])
    T.op("dve", lambda e: e.memset(Sst[:], 0.0), w=["Sst"])
    T.op("dve", lambda e: e.memset(Sbf[:], 0.0), w=["Sbf"])
    T.op("dve", lambda e: e.memset(mstate[:], 0.0), w=["mstate"])
    T.op("dve", lambda e: e.memset(histq[:], 0.0), w=["histq"])

    wslot = [0]

    wcache = {}

    def wprefetch(col0, ncols=512):
        if (col0, ncols) not in wcache:
            wcache[(col0, ncols)] = wload(col0, ncols)

    def wload(col0, ncols=512, src=None, nkc=16):
        if src is None and (col0, ncols) in wcache:
            return wcache.pop((col0, ncols))
        s = wslot[0] % len(wviews)
        wslot[0] = (s + 1) % len(wviews)
        key, base = wviews[s]
        view = base[:, 0:nkc * ncols].rearrange("p (a b) -> p a b", a=nkc)
        srcr = (src if src is not None else w_in_r)
        half = nkc // 2
        T.dma("pool", f"w{key}a", view[:, 0:half, :], srcr[:, 0:half, col0:col0 + ncols], w=[key + "a"])
        T.dma("pool", f"w{key}b", view[:, half:nkc, :], srcr[:, half:nkc, col0:col0 + ncols], w=[key + "b"])
        return view, [key + "a", key + "b"]

    WV3 = [("W0", A[:, 0:8192]), ("W1", A[:, 8192:16384]), ("W2", A[:, 16384:24576])]
    WV2 = WV3[0:2]
    WV4 = WV2 + [("W3", QR[:, 0:8192]), ("W4", QR[:, 8192:16384])]
    wviews = list(WV3)

    TB_MAIN = [(0, 512), (512, 512), (1024, 16)]
    TB_PRE = [(0, 512), (512, 512)]

    def fm_proj(Wv, Wk, cb_i, xsrc, xkeys, t0, n, M=128, c0=None):
        i, ps = PS()
        lo = cb_i * 128 if c0 is None else c0
        for kc in range(16):
            T.op("pe", lambda e, kc=kc: e.matmul(ps[0:M, 0:n], Wv[:, kc, lo:lo + M], xsrc[:, kc, t0:t0 + n],
                                                 start=(kc == 0), stop=(kc == 15)),
                 r=Wk + xkeys, w=[("ps", i)])
        return i, ps

    def mlstm_gates(xsrc, xkeys, is_main, scr):
        Wv = v3(wif, 16, 8)
        G = [scr[:, k * 2048:(k + 1) * 2048].bitcast(F32) for k in range(6)]
        itil, logf, bcs, gg, big, tmp = G
        for (t0, n) in TB_PRE:
            for which, dst in ((0, itil), (1, logf)):
                i, ps = PS()
                for kc in range(16):
                    T.op("pe", lambda e, kc=kc: e.matmul(ps[0:4, 0:n], Wv[:, kc, which * 4:which * 4 + 4], xsrc[:, kc, t0:t0 + n],
                                                         start=(kc == 0), stop=(kc == 15)), r=["wif"] + xkeys, w=[("ps", i)])
                if which == 0:
                    T.op("dve", lambda e: e.tensor_scalar(out=dst[0:4, t0:t0 + n], in0=ps[0:4, 0:n], scalar1=bi[:, 0:1], scalar2=None, op0=ALU.add),
                         r=[("ps", i), "bi"], w=["g_itil"])
                else:
                    T.op("act", lambda e: e.activation(out=dst[0:4, t0:t0 + n], in_=ps[0:4, 0:n], func=AF.Sigmoid, bias=bff[:, 0:1]),
                         r=[("ps", i), "bf"], w=["g_logf"])
        T.op("act", lambda e: e.activation(out=logf[0:4, 0:NP], in_=logf[0:4, 0:NP], func=AF.Ln), r=["g_logf"], w=["g_logf"])
        T.op("dve", lambda e: e.tensor_tensor_scan(out=bcs[0:4, 0:NP], data0=rmask[0:4, 0:NP], data1=logf[0:4, 0:NP], initial=0.0,
                                                   op0=ALU.mult, op1=ALU.add), r=["g_logf", "rmask"], w=["g_b"])
        T.op("dve", lambda e: e.tensor_tensor(out=gg[0:4, 0:NP], in0=itil[0:4, 0:NP], in1=bcs[0:4, 0:NP], op=ALU.subtract),
             r=["g_itil", "g_b"], w=["g_g"])
        st = tmp
        T.op("dve", lambda e: e.tensor_reduce(out=st[0:4, 0:NCH], in_=gg[0:4, 0:NP].rearrange("p (c t) -> p c t", c=NCH), axis=AX.X, op=ALU.max),
             r=["g_g"], w=["g_st"])
        T.op("dve", lambda e: e.tensor_copy(out=st[0:4, 8:16], in_=bcs[0:4, 127:NP:128]), r=["g_b", "g_st"], w=["g_st"])
        T.op("dve", lambda e: e.tensor_tensor_scan(out=st[0:4, 16:24], data0=st[0:4, 0:8], data1=st[0:4, 8:16], initial=mstate[0:4, 0:1],
                                                   op0=ALU.max, op1=ALU.add), r=["g_st", "mstate"], w=["g_st"])
        T.op("dve", lambda e: e.tensor_copy(out=st[0:4, 24:25], in_=mstate[0:4, 0:1]), r=["g_st", "mstate"], w=["g_st"])
        T.op("dve", lambda e: e.tensor_copy(out=st[0:4, 25:32], in_=st[0:4, 16:23]), r=["g_st"], w=["g_st"])
        T.op("dve", lambda e: e.tensor_copy(out=mstate[0:4, 0:1], in_=st[0:4, 23:24]), r=["g_st"], w=["mstate"])
        T.op("dve", lambda e: e.tensor_tensor(out=st[0:4, 40:48], in0=st[0:4, 8:16], in1=st[0:4, 24:32], op=ALU.add), r=["g_st"], w=["g_st"])
        T.op("dve", lambda e: e.tensor_tensor(out=st[0:4, 40:48], in0=st[0:4, 40:48], in1=st[0:4, 16:24], op=ALU.subtract), r=["g_st"], w=["g_st"])
        T.op("act", lambda e: e.activation(out=st[0:4, 32:40], in_=st[0:4, 40:48], func=AF.Exp), r=["g_st"], w=["g_st"])
        big3 = big[:, 0:NP].rearrange("p (c t) -> p c t", c=NCH)
        T.op("dve", lambda e: e.memset(big[0:100, 0:NP], 0.0), w=["g_big"])
        mprev_b = st[0:4, 24:32].unsqueeze(2).to_broadcast([4, NCH, 128])
        dec_b = st[0:4, 32:40].unsqueeze(2).to_broadcast([4, NCH, 128])
        gg3 = gg[0:4, 0:NP].rearrange("p (c t) -> p c t", c=NCH)
        b3 = bcs[0:4, 0:NP].rearrange("p (c t) -> p c t", c=NCH)
        T.op("dve", lambda e: e.tensor_tensor(out=gg3, in0=gg3, in1=mprev_b, op=ALU.subtract), r=["g_g", "g_st"], w=["g_g"])
        T.op("act", lambda e: e.activation(out=big[0:4, 0:NP], in_=gg[0:4, 0:NP], func=AF.Exp), r=["g_g", "g_big"], w=["g_big"])
        T.op("dve", lambda e: e.scalar_tensor_tensor(out=big3[32:36], in0=big3[0:4], scalar=1.0 / 16.0, in1=dec_b, op0=ALU.mult, op1=ALU.mult),
             r=["g_big", "g_st"], w=["g_big"])
        T.op("dve", lambda e: e.tensor_tensor(out=b3, in0=b3, in1=mprev_b, op=ALU.add), r=["g_b", "g_st"], w=["g_b"])
        T.op("act", lambda e: e.activation(out=big[64:68, 0:NP], in_=bcs[0:4, 0:NP], func=AF.Exp, scale=-1.0), r=["g_b", "g_big"], w=["g_big"])
        T.op("dve", lambda e: e.tensor_copy(out=big3[96:100], in_=dec_b), r=["g_st", "g_big"], w=["g_big"])
        tok3 = v3(tokT, NCH, 100)
        for c4 in range(0, NCH, 4):
            i, ps = PS()
            for c in range(c4, c4 + 4):
                T.op("pe", lambda e, c=c: e.transpose(ps[:, (c - c4) * 100:(c - c4) * 100 + 100], big[0:100, c * 128:(c + 1) * 128], ident_f[0:100, 0:100]),
                     r=["g_big", "ident_f"], w=[("ps", i)])
            T.op("act", lambda e: e.activation(out=tokT[:, c4 * 100:(c4 + 4) * 100], in_=ps[:, 0:400], func=AF.Copy), r=[("ps", i)], w=["tokT"])
        return tok3

    pre_rot = [0]
    acc_rot = [0]
    preb = [R[:, k * 1040:(k + 1) * 1040].bitcast(F32) for k in range(3)]

    def conv_block(i, ps, blk, t0, n, dst, dstkey, is_main, tbidx, save_pconv):
        k = pre_rot[0]; pre_rot[0] = (k + 1) % 3
        P = preb[k]; pk = f"pre{k}"
        T.op("act", lambda e: e.activation(out=P[:, 3:3 + n], in_=ps[:, 0:n], func=AF.Copy), r=[("ps", i)], w=[pk])
        if tbidx == 0:
            T.op("dve", lambda e: e.tensor_copy(out=P[:, 0:3], in_=histq[:, blk * 3:blk * 3 + 3]), r=["histq", pk], w=[pk])
        else:
            kp = (k + 2) % 3
            T.op("dve", lambda e: e.tensor_copy(out=P[:, 0:3], in_=preb[kp][:, 512:515]), r=[f"pre{kp}", pk], w=[pk])
        ka = acc_rot[0]; acc_rot[0] = (ka + 1) % 2
        acc = hn_acc[ka]
        ak = f"acc{ka}"
        c4 = blk * 4
        T.op("dve", lambda e: e.tensor_scalar(out=acc[:, 0:n], in0=P[:, 3:3 + n], scalar1=cw[:, c4 + 3:c4 + 4], scalar2=cb[:, blk:blk + 1],
                                              op0=ALU.mult, op1=ALU.add), r=[pk, "cw", "cb"], w=[ak])
        for j in (2, 1, 0):
            T.op("dve", lambda e, j=j: e.scalar_tensor_tensor(out=acc[:, 0:n], in0=P[:, j:j + n], scalar=cw[:, c4 + j:c4 + j + 1], in1=acc[:, 0:n],
                                                              op0=ALU.mult, op1=ALU.add), r=[pk, ak], w=[ak])
        T.op("act", lambda e: e.activation(out=dst, in_=acc[:, 0:n], func=AF.Silu), r=[ak], w=[dstkey])
        if tbidx == 1:
            if save_pconv:
                T.op("dve", lambda e: e.tensor_copy(out=pconv_sb[:, blk * 3:blk * 3 + 3], in_=P[:, 512:515]), r=[pk], w=["pconv_sb"])
            else:
                T.op("dve", lambda e: e.tensor_scalar(out=histq[:, blk * 3:blk * 3 + 3], in0=P[:, 512:515], scalar1=flag[:, 0:1], scalar2=None, op0=ALU.mult),
                     r=[pk, "flag"], w=["histq"])


    hn_acc = accb

    conv_in4 = conv_in[:, :].rearrange("p (b j s) -> p b j s", b=16, j=3)
    sconv4 = sconv_sb[:, :].rearrange("p (b j s) -> p b j s", b=16, j=3)

    def conv_block_sample(i, ps, blk, dst, dstkey):
        acc = sacc_t
        c4 = blk * 4
        T.op("dve", lambda e: e.tensor_scalar(out=acc[:, 0:16], in0=ps[:, 0:16], scalar1=cw[:, c4 + 3:c4 + 4], scalar2=cb[:, blk:blk + 1],
                                              op0=ALU.mult, op1=ALU.add), r=[("ps", i)], w=["sacc"])
        for j in (2, 1, 0):
            T.op("dve", lambda e, j=j: e.scalar_tensor_tensor(out=acc[:, 0:16], in0=conv_in4[:, blk, j, :], scalar=cw[:, c4 + j:c4 + j + 1], in1=acc[:, 0:16],
                                                              op0=ALU.mult, op1=ALU.add), r=["conv_in", "sacc"], w=["sacc"])
        T.op("act", lambda e: e.activation(out=dst, in_=acc[:, 0:16], func=AF.Silu), r=["sacc"], w=[dstkey])
        T.op("act", lambda e: e.activation(out=sconv4[:, blk, 2, :], in_=ps[:, 0:16], func=AF.Copy), r=[("ps", i)], w=["sconv_sb"])
        T.op("dve", lambda e: e.tensor_copy(out=sconv4[:, blk, 0:2, :], in_=conv_in4[:, blk, 1:3, :]), r=["conv_in"], w=["sconv_sb"])

    qT = Q[:, 0:4 * NM].rearrange("p (a b) -> p a b", a=4)
    kT = Q[:, 4 * NM:8 * NM].rearrange("p (a b) -> p a b", a=4)
    vtok = Q[:, 8 * NM:8 * NM + 9 * 512].rearrange("p (a b) -> p a b", a=9)
    gq = Q[:, 0:2 * NM].rearrange("p (a b) -> p a b", a=2)
    gk = Q[:, 2 * NM:4 * NM].rearrange("p (a b) -> p a b", a=2)
    gkh = Q[:, 4 * NM:6 * NM].rearrange("p (a b) -> p a b", a=2)

    def post_mlstm(Hap, Hkeys, theta_ap, theta_keys, npart, k, dst_fn):
        sm = small[k]; sk = f"small{k}"
        T.op("dve", lambda e: e.tensor_scalar(out=sm[0:npart, 6:7], in0=Hap[:, 256:257], scalar1=-1.0, scalar2=None, op0=ALU.mult),
             r=Hkeys, w=[sk])
        T.op("dve", lambda e: e.scalar_tensor_tensor(out=sm[0:npart, 0:1], in0=Hap[:, 256:257], scalar=theta_ap, in1=sm[0:npart, 6:7], op0=ALU.max, op1=ALU.max),
             r=Hkeys + theta_keys + [sk], w=[sk])
        T.op("dve", lambda e: e.bn_stats(out=sm[0:npart, 8:14], in_=Hap[:, 0:256]), r=Hkeys + [sk], w=[sk])
        T.op("dve", lambda e: e.bn_aggr(out=sm[0:npart, 2:4], in_=sm[0:npart, 8:14]), r=[sk], w=[sk])
        T.op("dve", lambda e: e.tensor_tensor(out=sm[0:npart, 1:2], in0=sm[0:npart, 0:1], in1=sm[0:npart, 0:1], op=ALU.mult), r=[sk], w=[sk])
        T.op("dve", lambda e: e.scalar_tensor_tensor(out=sm[0:npart, 1:2], in0=sm[0:npart, 1:2], scalar=EPS, in1=sm[0:npart, 3:4], op0=ALU.mult, op1=ALU.add),
             r=[sk], w=[sk])
        T.op("act", lambda e: e.activation(out=sm[0:npart, 4:5], in_=sm[0:npart, 1:2], func=AF.Ln), r=[sk], w=[sk])
        T.op("act", lambda e: e.activation(out=sm[0:npart, 5:6], in_=sm[0:npart, 4:5], func=AF.Exp, scale=-0.5), r=[sk], w=[sk])
        T.op("dve", lambda e: e.tensor_scalar(out=hnb[k][0:npart, :], in0=Hap[:, 0:256], scalar1=sm[0:npart, 2:3], scalar2=sm[0:npart, 5:6],
                                              op0=ALU.subtract, op1=ALU.mult), r=Hkeys + [sk], w=[f"hnb{k}"])
        dst_fn(k)

    def to_Y(k, npart, blk0, t0):
        i, ps = PS()
        psb = ps[:, :].bitcast(BF16)
        for dh in range(2):
            T.op("pe", lambda e, dh=dh: e.transpose(psb[:, dh * 128:dh * 128 + npart], hnb[k][0:npart, dh * 128:(dh + 1) * 128], ident_bf[0:npart, 0:npart]),
                 r=[f"hnb{k}", "ident_bf"], w=[("ps", i)])
        T.op("act", lambda e: e.activation(out=Y3[:, blk0:blk0 + 2, t0:t0 + npart], in_=psb[:, 0:256].rearrange("p (a b) -> p a b", a=2)[:, :, 0:npart], func=AF.Copy),
             r=[("ps", i)], w=["Y"])

    def mlstm_segment(hp, is_main, tok3, bg=None, nxt=None, bg2=None, bgn=2):
        def tick():
            if bg2 is not None:
                bg2(bgn)
        xsrc = X3 if is_main else xTp3
        xkeys = XK if is_main else XPK
        tbs = TB_MAIN if is_main else TB_PRE
        heads = (2 * hp, 2 * hp + 1)
        if is_main:
            Wq, Wqk = wload(C_Q + hp * 512)
            for cbi in range(4):
                blk = hp * 4 + cbi
                i, ps = PS()
                for kc in range(16):
                    T.op("pe", lambda e, kc=kc: e.matmul(ps[:, 0:4], Wq[:, kc, cbi * 128:(cbi + 1) * 128], v3(xTp_tail, 16, 4)[:, kc, :],
                                                         start=(kc == 0), stop=(kc == 15)), r=Wqk + ["xTp_tail"], w=[("ps", i)])
                T.op("dve", lambda e: e.tensor_scalar(out=histq[:, blk * 3:blk * 3 + 3], in0=ps[:, 1:4], scalar1=flag[:, 0:1], scalar2=None, op0=ALU.mult),
                     r=[("ps", i), "flag"], w=["histq"])
                for tbi, (t0, n) in enumerate(tbs):
                    i, ps = fm_proj(Wq, Wqk, cbi, xsrc, xkeys, t0, n)
                    if n == 16:
                        conv_block_sample(i, ps, blk, qT[:, cbi, t0:t0 + n], "qT")
                    else:
                        conv_block(i, ps, blk, t0, n, qT[:, cbi, t0:t0 + n], "qT", is_main, tbi, True)
                    tick()
        Wk_, Wkk = wload(C_K + hp * 512)
        for cbi in range(4):
            blk = 8 + hp * 4 + cbi
            for tbi, (t0, n) in enumerate(tbs):
                i, ps = fm_proj(Wk_, Wkk, cbi, xsrc, xkeys, t0, n)
                if n == 16:
                    conv_block_sample(i, ps, blk, kT[:, cbi, t0:t0 + n], "kT")
                else:
                    conv_block(i, ps, blk, t0, n, kT[:, cbi, t0:t0 + n], "kT", is_main, tbi, is_main)
                tick()
        Wv_, Wvk = wload(C_V + hp * 512)
        ntile = 9 if is_main else 8
        for tl in range(ntile):
            m = 128 if tl < 8 else 16
            i, ps = PS()
            for kc in range(16):
                T.op("pe", lambda e, kc=kc: e.matmul(ps[0:m, 0:512], xsrc[:, kc, tl * 128:tl * 128 + m], Wv_[:, kc, :], start=(kc == 0), stop=(kc == 15)),
                     r=Wvk + xkeys, w=[("ps", i)])
            T.op("act", lambda e: e.activation(out=vtok[0:m, tl, :], in_=ps[0:m, 0:512], func=AF.Copy), r=[("ps", i)], w=["vtok"])
            tick()
        if nxt is not None:
            nxt()
        items = [(c, hl, h) for c in range(NCH) for hl, h in enumerate(heads)]
        held = {}

        def stA(i):
            c, hl, h = items[i]; k = i % 2
            ts = slice(c * 128, (c + 1) * 128)
            if is_main:
                i_s, ps_s = PS()
                for dh in range(2):
                    T.op("pe", lambda e, dh=dh: e.matmul(ps_s[:, 0:128], kT[:, hl * 2 + dh, ts], qT[:, hl * 2 + dh, ts], start=(dh == 0), stop=(dh == 1)),
                         r=["kT", "qT"], w=[("ps", i_s)])
            i_t, ps_t = PS()
            pst = ps_t[:, :].bitcast(BF16)
            for dh in range(2):
                T.op("pe", lambda e, dh=dh: e.transpose(pst[:, dh * 128:(dh + 1) * 128], kT[:, hl * 2 + dh, ts], ident_bf[:]), r=["kT", "ident_bf"], w=[("ps", i_t)])
            if is_main:
                T.op("dve", lambda e: e.tensor_tensor(out=Sm[k][:], in0=ps_s[:, 0:128], in1=maskA[:], op=ALU.mult), r=[("ps", i_s), "maskA"], w=[f"Sm{k}"])
            T.op("act", lambda e: e.activation(out=ktok[k][:], in_=pst[:, 0:256], func=AF.Copy), r=[("ps", i_t)], w=[f"ktok{k}"])
            if is_main:
                T.op("pool", lambda e: e.tensor_scalar(out=vt[k][:, 0:256], in0=vtok[:, c, hl * 256:(hl + 1) * 256], scalar1=tok3[:, c, h:h + 1], scalar2=0.0, op0=ALU.mult, op1=ALU.add),
                     r=["vtok", "tokT"], w=[f"vt{k}"])
                T.op("pool", lambda e: e.tensor_copy(out=vt[k][:, 256:257], in_=tok3[:, c, h:h + 1]), r=["tokT"], w=[f"vt{k}b"])
            T.op("pool", lambda e: e.tensor_scalar(out=vh[k][:, 0:256], in0=vtok[:, c, hl * 256:(hl + 1) * 256], scalar1=tok3[:, c, 32 + h:33 + h], scalar2=0.0, op0=ALU.mult, op1=ALU.add),
                 r=["vtok", "tokT"], w=[f"vh{k}"])
            T.op("pool", lambda e: e.tensor_copy(out=vh[k][:, 256:257], in_=tok3[:, c, 32 + h:33 + h]), r=["tokT"], w=[f"vh{k}b"])

        def stB(i):
            c, hl, h = items[i]; k = i % 2
            ts = slice(c * 128, (c + 1) * 128)
            if is_main:
                i_h, ps_h = PS()
                T.op("pe", lambda e: e.matmul(ps_h[:, 0:257], Sm[k][:], vt[k][:], start=True, stop=False), r=[f"Sm{k}", f"vt{k}", f"vt{k}b"], w=[("ps", i_h)])
                for dh in range(2):
                    T.op("pe", lambda e, dh=dh: e.matmul(ps_h[:, 0:257], qT[:, hl * 2 + dh, ts], Cbf[:, (h * 2 + dh) * 257:(h * 2 + dh + 1) * 257], start=False, stop=(dh == 1)),
                         r=["qT", f"Cbf{h}"], w=[("ps", i_h)])
            for dh in range(2):
                i_u, ps_u = PS()
                sl = slice((h * 2 + dh) * 257, (h * 2 + dh + 1) * 257)
                T.op("pe", lambda e, dh=dh: e.matmul(ps_u[:, 0:257], ktok[k][:, dh * 128:(dh + 1) * 128], vh[k][:], start=True, stop=True),
                     r=[f"ktok{k}", f"vh{k}", f"vh{k}b"], w=[("ps", i_u)])
                T.op("dve", lambda e, sl=sl: e.scalar_tensor_tensor(out=Caug[:, sl], in0=Caug[:, sl], scalar=tok3[:, c, 96 + h:97 + h], in1=ps_u[:, 0:257], op0=ALU.mult, op1=ALU.add),
                     r=[("ps", i_u), "tokT", f"Caug{h}"], w=[f"Caug{h}"])
                if is_main:
                    T.op("act", lambda e, sl=sl: e.activation(out=Cbf[:, sl], in_=Caug[:, sl], func=AF.Copy), r=[f"Caug{h}"], w=[f"Cbf{h}"])
            if is_main:
                post_mlstm(ps_h[:, 0:257], [("ps", i_h)], tok3[:, c, 64 + h:65 + h], ["tokT"], 128, k, lambda k: None)

        def stC(i):
            c, hl, h = items[i]; k = i % 2
            to_Y(k, 128, h * 2, c * 128)

        NI = len(items)
        stA(0)
        for i in range(NI):
            if i + 1 < NI:
                stA(i + 1)
            stB(i)
            if is_main and i >= 1:
                stC(i - 1)
            if bg is not None:
                bg(4)
        if is_main:
            stC(NI - 1)
        return

    def post_gla(Oap, Okeys, npart, k, dst_fn):
        sm = small[k]; sk = f"small{k}"
        T.op("dve", lambda e: e.bn_stats(out=sm[0:npart, 8:14], in_=Oap), r=Okeys + [sk], w=[sk])
        T.op("dve", lambda e: e.bn_aggr(out=sm[0:npart, 2:4], in_=sm[0:npart, 8:14]), r=[sk], w=[sk])
        T.op("dve", lambda e: e.tensor_tensor(out=sm[0:npart, 1:2], in0=sm[0:npart, 2:3], in1=sm[0:npart, 2:3], op=ALU.mult), r=[sk], w=[sk])
        T.op("dve", lambda e: e.scalar_tensor_tensor(out=sm[0:npart, 1:2], in0=sm[0:npart, 1:2], scalar=EPS, in1=sm[0:npart, 3:4], op0=ALU.add, op1=ALU.add),
             r=[sk], w=[sk])
        T.op("act", lambda e: e.activation(out=sm[0:npart, 4:5], in_=sm[0:npart, 1:2], func=AF.Ln), r=[sk], w=[sk])
        T.op("act", lambda e: e.activation(out=sm[0:npart, 5:6], in_=sm[0:npart, 4:5], func=AF.Exp, scale=-0.5), r=[sk], w=[sk])
        T.op("dve", lambda e: e.tensor_scalar(out=hnb[k][0:npart, :], in0=Oap, scalar1=sm[0:npart, 5:6], scalar2=None, op0=ALU.mult), r=Okeys + [sk], w=[f"hnb{k}"])
        dst_fn(k)

    def gla_bg(xsrc, xkeys, tbs):
        Wv = v3(wbg, 16, 16)
        for (t0, n) in tbs:
            i, ps = PS()
            for kc in range(16):
                T.op("pe", lambda e, kc=kc: e.matmul(ps[0:16, 0:n], Wv[:, kc, 0:16], xsrc[:, kc, t0:t0 + n], start=(kc == 0), stop=(kc == 15)),
                     r=["wbg"] + xkeys, w=[("ps", i)])
            T.op("act", lambda e: e.activation(out=bgT[0:16, t0:t0 + n], in_=ps[0:16, 0:n], func=AF.Copy), r=[("ps", i)], w=["bgT"])

    gvtok = vtok
    stq3 = stq[:, :].rearrange("p (a b) -> p a b", a=4)
    stk3 = stk[:, :].rearrange("p (a b) -> p a b", a=4)

    def stash_mlstm():
        T.op("act", lambda e: e.activation(out=stq3, in_=qT[:, :, NP:NM], func=AF.Copy), r=["qT"], w=["stq"])
        T.op("act", lambda e: e.activation(out=stk3, in_=kT[:, :, NP:NM], func=AF.Copy), r=["kT"], w=["stk"])
        T.op("act", lambda e: e.activation(out=stv[:, :], in_=vtok[0:16, 8, :], func=AF.Copy), r=["vtok"], w=["stv"])

    def stash_gla():
        T.op("act", lambda e: e.activation(out=stq3[:, 0:2, :], in_=gq[:, :, NP:NM], func=AF.Copy), r=["gq"], w=["stq"])
        T.op("act", lambda e: e.activation(out=stk3[:, 0:2, :], in_=gk[:, :, NP:NM], func=AF.Copy), r=["gk"], w=["stk"])
        T.op("act", lambda e: e.activation(out=stk3[:, 2:4, :], in_=gkh[:, :, NP:NM], func=AF.Copy), r=["gkh"], w=["stk"])
        T.op("act", lambda e: e.activation(out=stv[:, :], in_=gvtok[0:16, 8, :], func=AF.Copy), r=["vtok"], w=["stv"])
    bcb = [R[:, k * 2080:(k + 1) * 2080].bitcast(F32) for k in range(2)]

    def gla_segment(hp, is_main, bg=None, nxt=None, bg2=None, bgn=2):
        def tick():
            if bg2 is not None:
                bg2(bgn)
        xsrc = X3 if is_main else xTp3
        xkeys = XK if is_main else XPK
        tbs = TB_MAIN if is_main else TB_PRE
        NT = NM if is_main else NP
        heads = (2 * hp, 2 * hp + 1)
        for hl, h in enumerate(heads):
            bcs = bcb[hl]; bk = f"bcs{hl}"
            for (t0, n) in tbs:
                i, ps = PS()
                T.op("pe", lambda e: e.matmul(ps[:, 0:n], wgu[0:16, h * 128:(h + 1) * 128], bgT[0:16, t0:t0 + n], start=True, stop=True),
                     r=["wgu", "bgT"], w=[("ps", i)])
                T.op("act", lambda e: e.activation(out=bcs[:, t0:t0 + n], in_=ps[:, 0:n], func=AF.Sigmoid, bias=bgate[:, h:h + 1]), r=[("ps", i), "bgate"], w=[bk])
            T.op("act", lambda e: e.activation(out=bcs[:, 0:NT], in_=bcs[:, 0:NT], func=AF.Ln), r=[bk], w=[bk])
            T.op("dve", lambda e: e.tensor_tensor_scan(out=bcs[:, 0:NT], data0=rmask[:, 0:NT], data1=bcs[:, 0:NT], initial=0.0, op0=ALU.mult, op1=ALU.add),
                 r=[bk, "rmask"], w=[bk])
            T.op("dve", lambda e: e.tensor_scalar(out=last16[:, hl * NCH:(hl + 1) * NCH], in0=bcs[:, 127:NP:128], scalar1=1.0 / 16.0, scalar2=None, op0=ALU.mult),
                 r=[bk], w=["last16"])
            T.op("act", lambda e: e.activation(out=elast[:, h * NCH:(h + 1) * NCH], in_=bcs[:, 127:NP:128], func=AF.Exp, scale=1.0 / 16.0), r=[bk], w=["elast"])
            if is_main:
                T.op("act", lambda e: e.activation(out=elast_s[:, h * 16:(h + 1) * 16], in_=bcs[:, NP:NM], func=AF.Exp, scale=1.0 / 16.0), r=[bk], w=["elast_s"])
        if is_main:
            Wq, Wqk = wload(C_BQ + hp * 256, ncols=256)
            for hl, h in enumerate(heads):
                for (t0, n) in tbs:
                    i, ps = fm_proj(Wq, Wqk, hl, xsrc, xkeys, t0, n)
                    k = acc_rot[0]; acc_rot[0] = (k + 1) % 2
                    T.op("act", lambda e: e.activation(out=accb[k][:, 0:n], in_=bcb[hl][:, t0:t0 + n], func=AF.Exp, scale=1.0 / 16.0), r=[f"bcs{hl}"], w=[f"acc{k}"])
                    T.op("dve", lambda e: e.scalar_tensor_tensor(out=gq[:, hl, t0:t0 + n], in0=ps[:, 0:n], scalar=float(128 ** -0.5), in1=accb[k][:, 0:n], op0=ALU.mult, op1=ALU.mult),
                         r=[("ps", i), f"acc{k}"], w=["gq"])
                    tick()
        Wk_, Wkk = wload(C_BK + hp * 256, ncols=256)
        for hl, h in enumerate(heads):
            for (t0, n) in tbs:
                i, ps = fm_proj(Wk_, Wkk, hl, xsrc, xkeys, t0, n)
                if is_main:
                    k = acc_rot[0]; acc_rot[0] = (k + 1) % 2
                    T.op("act", lambda e: e.activation(out=accb[k][:, 0:n], in_=bcb[hl][:, t0:t0 + n], func=AF.Exp, scale=-1.0 / 16.0), r=[f"bcs{hl}"], w=[f"acc{k}"])
                    T.op("dve", lambda e: e.tensor_tensor(out=gk[:, hl, t0:t0 + n], in0=ps[:, 0:n], in1=accb[k][:, 0:n], op=ALU.mult), r=[("ps", i), f"acc{k}"], w=["gk"])
                if n == 16:
                    T.op("act", lambda e: e.activation(out=gkh[:, hl, t0:t0 + n], in_=ps[:, 0:n], func=AF.Copy), r=[("ps", i)], w=["gkh"])
                else:
                    k = acc_rot[0]; acc_rot[0] = (k + 1) % 2
                    for cc in range(n // 128):
                        c = (t0 + cc * 128) // 128
                        T.op("act", lambda e, cc=cc, c=c: e.activation(out=accb[k][:, cc * 128:(cc + 1) * 128], in_=bcb[hl][:, t0 + cc * 128:t0 + (cc + 1) * 128], func=AF.Exp,
                                                                       scale=-1.0 / 16.0, bias=last16[:, hl * NCH + c:hl * NCH + c + 1]), r=[f"bcs{hl}", "last16"], w=[f"acc{k}"])
                    T.op("dve", lambda e: e.tensor_tensor(out=gkh[:, hl, t0:t0 + n], in0=ps[:, 0:n], in1=accb[k][:, 0:n], op=ALU.mult), r=[("ps", i), f"acc{k}"], w=["gkh"])
                tick()
        Wv_, Wvk = wload(C_BV + hp * 512)
        ntile = 9 if is_main else 8
        for tl in range(ntile):
            m = 128 if tl < 8 else 16
            i, ps = PS()
            for kc in range(16):
                T.op("pe", lambda e, kc=kc: e.matmul(ps[0:m, 0:512], xsrc[:, kc, tl * 128:tl * 128 + m], Wv_[:, kc, :], start=(kc == 0), stop=(kc == 15)),
                     r=Wvk + xkeys, w=[("ps", i)])
            T.op("act", lambda e: e.activation(out=gvtok[0:m, tl, :], in_=ps[0:m, 0:512], func=AF.Copy), r=[("ps", i)], w=["vtok"])
            tick()
        if nxt is not None:
            nxt()
        items = [(c, hl, h) for c in range(NCH) for hl, h in enumerate(heads)]

        def stA(i):
            c, hl, h = items[i]; k = i % 2
            ts = slice(c * 128, (c + 1) * 128)
            if is_main:
                i_s, ps_s = PS()
                T.op("pe", lambda e: e.matmul(ps_s[:, 0:128], gk[:, hl, ts], gq[:, hl, ts], start=True, stop=True), r=["gk", "gq"], w=[("ps", i_s)])
            i_t, ps_t = PS()
            pst = ps_t[:, :].bitcast(BF16)
            T.op("pe", lambda e: e.transpose(pst[:, 0:128], gkh[:, hl, ts], ident_bf[:]), r=["gkh", "ident_bf"], w=[("ps", i_t)])
            if is_main:
                T.op("dve", lambda e: e.tensor_tensor(out=Sm[k][:], in0=ps_s[:, 0:128], in1=maskB[:], op=ALU.mult), r=[("ps", i_s), "maskB"], w=[f"Sm{k}"])
            T.op("act", lambda e: e.activation(out=ktok[k][:, 0:128], in_=pst[:, 0:128], func=AF.Copy), r=[("ps", i_t)], w=[f"ktok{k}"])

        def stB(i):
            c, hl, h = items[i]; k = i % 2
            ts = slice(c * 128, (c + 1) * 128)
            ssl = slice(h * 256, (h + 1) * 256)
            if is_main:
                i_h, ps_h = PS()
                T.op("pe", lambda e: e.matmul(ps_h[:, 0:256], Sm[k][:], gvtok[:, c, hl * 256:(hl + 1) * 256], start=True, stop=False), r=[f"Sm{k}", "vtok"], w=[("ps", i_h)])
                T.op("pe", lambda e: e.matmul(ps_h[:, 0:256], gq[:, hl, ts], Sbf[:, ssl], start=False, stop=True), r=["gq", f"Sbf{h}"], w=[("ps", i_h)])
            i_u, ps_u = PS()
            T.op("pe", lambda e: e.matmul(ps_u[:, 0:256], ktok[k][:, 0:128], gvtok[:, c, hl * 256:(hl + 1) * 256], start=True, stop=True), r=[f"ktok{k}", "vtok"], w=[("ps", i_u)])
            T.op("dve", lambda e: e.scalar_tensor_tensor(out=Sst[:, ssl], in0=Sst[:, ssl], scalar=elast[:, h * NCH + c:h * NCH + c + 1], in1=ps_u[:, 0:256], op0=ALU.mult, op1=ALU.add),
                 r=[("ps", i_u), "elast", f"Sst{h}"], w=[f"Sst{h}"])
            if is_main:
                T.op("act", lambda e: e.activation(out=Sbf[:, ssl], in_=Sst[:, ssl], func=AF.Copy), r=[f"Sst{h}"], w=[f"Sbf{h}"])
            if is_main:
                post_gla(ps_h[:, 0:256], [("ps", i_h)], 128, k, lambda k: None)

        def stC(i):
            c, hl, h = items[i]; k = i % 2
            to_Y(k, 128, 8 + h * 2, c * 128)

        NI = len(items)
        stA(0)
        for i in range(NI):
            if i + 1 < NI:
                stA(i + 1)
            stB(i)
            if is_main and i >= 1:
                stC(i - 1)
            if bg is not None:
                bg(2)
        if is_main:
            stC(NI - 1)

    def sample_gates():
        Wv = v3(wif, 16, 8)
        T.dma("sp", "c_m_in", stab[:, 8:12], m_in_d[:, :], w=["stab_m"])
        i, ps = PS()
        for kc in range(16):
            T.op("pe", lambda e, kc=kc: e.matmul(ps[0:16, 0:8], X3[:, kc, NP:NM], Wv[:, kc, 0:8], start=(kc == 0), stop=(kc == 15)), r=["wif"] + XK, w=[("ps", i)])
        T.op("dve", lambda e: e.tensor_tensor(out=stab[:, 0:4], in0=ps[0:16, 0:4], in1=bi_row[:], op=ALU.add), r=[("ps", i), "bi_row"], w=["stab"])
        T.op("dve", lambda e: e.tensor_tensor(out=stab[:, 4:8], in0=ps[0:16, 4:8], in1=bf_row[:], op=ALU.add), r=[("ps", i), "bf_row", "stab"], w=["stab"])
        T.op("act", lambda e: e.activation(out=stab[:, 4:8], in_=stab[:, 4:8], func=AF.Sigmoid), r=["stab"], w=["stab"])
        T.op("act", lambda e: e.activation(out=stab[:, 4:8], in_=stab[:, 4:8], func=AF.Ln), r=["stab"], w=["stab"])
        T.op("dve", lambda e: e.tensor_tensor(out=stab[:, 12:16], in0=stab[:, 4:8], in1=stab[:, 8:12], op=ALU.add), r=["stab", "stab_m"], w=["stab"])
        T.op("dve", lambda e: e.tensor_tensor(out=stab[:, 16:20], in0=stab[:, 12:16], in1=stab[:, 0:4], op=ALU.max), r=["stab"], w=["stab"])
        T.op("dve", lambda e: e.tensor_tensor(out=stab[:, 20:24], in0=stab[:, 0:4], in1=stab[:, 16:20], op=ALU.subtract), r=["stab"], w=["stab"])
        T.op("dve", lambda e: e.tensor_tensor(out=stab[:, 24:28], in0=stab[:, 12:16], in1=stab[:, 16:20], op=ALU.subtract), r=["stab"], w=["stab"])
        T.op("act", lambda e: e.activation(out=stab[:, 20:28], in_=stab[:, 20:28], func=AF.Exp), r=["stab"], w=["stab"])
        T.op("act", lambda e: e.activation(out=stab[:, 28:32], in_=stab[:, 16:20], func=AF.Exp, scale=-1.0), r=["stab"], w=["stab"])
        T.op("dve", lambda e: e.tensor_scalar(out=stab[:, 40:44], in0=stab[:, 20:24], scalar1=1.0 / 16.0, scalar2=None, op0=ALU.mult), r=["stab"], w=["stab"])
        T.dma("sp", "o_sm", sm_d[:, :], stab[:, 16:20], r=["stab"])
        dd = junk16[:, 0:64].rearrange("p (s h) -> p s h", s=16)
        T.op("dve", lambda e: e.tensor_tensor(out=dd, in0=diag16[:, :].unsqueeze(2).to_broadcast([16, 16, 4]), in1=stab[:, 24:28].unsqueeze(1).to_broadcast([16, 16, 4]), op=ALU.mult),
             r=["diag16", "stab"], w=["junk16"])
        i, ps = PS()
        T.op("pe", lambda e: e.matmul(ps[:, 0:64], ones16[:, :], junk16[:, 0:64], start=True, stop=True), r=["ones16", "junk16"], w=[("ps", i)])
        T.op("act", lambda e: e.activation(out=decbc[:, :], in_=ps[:, 0:64], func=AF.Copy), r=[("ps", i)], w=["decbc"])

    slot_rot = [0]

    def make_bg(gen):
        state = {"done": False}

        def bg(n):
            for _ in range(n):
                if state["done"]:
                    return
                try:
                    next(gen)
                except StopIteration:
                    state["done"] = True

        def drain():
            while not state["done"]:
                bg(1)
        bg.drain = drain
        return bg

    def mlstm_samples(hp):
        for hl, h in enumerate((2 * hp, 2 * hp + 1)):
            i, ps = PS()
            psb = ps[:, :].bitcast(BF16)
            for dh in range(2):
                T.op("pe", lambda e, dh=dh: e.transpose(psb[0:16, dh * 128:(dh + 1) * 128], stq3[:, hl * 2 + dh, :], ident_bf[:]), r=["stq", "ident_bf"], w=[("ps", i)])
                T.op("pe", lambda e, dh=dh: e.transpose(psb[0:16, 256 + dh * 128:256 + (dh + 1) * 128], stk3[:, hl * 2 + dh, :], ident_bf[:]), r=["stk", "ident_bf"], w=[("ps", i)])
            T.op("act", lambda e: e.activation(out=qk_s[:, :], in_=psb[0:16, 0:512], func=AF.Copy), r=[("ps", i)], w=["qk_s"])
            T.op("dve", lambda e: e.scalar_tensor_tensor(out=junk16[:, :], in0=qk_s[:, 0:256], scalar=1.0 / 16.0, in1=qk_s[:, 256:512], op0=ALU.mult, op1=ALU.mult,
                                                         accum_out=stab[:, 32 + h:33 + h]), r=["qk_s", "junk16"], w=["junk16", "stab_q"])
            T.op("dve", lambda e: e.tensor_tensor(out=stab[:, 36 + h:37 + h], in0=stab[:, 32 + h:33 + h], in1=stab[:, 20 + h:21 + h], op=ALU.mult), r=["stab_q", "stab"], w=["stab_q"])
            qm3 = qm_s[:, :].rearrange("p (d s j) -> p d s j", d=2, s=16)
            for dh in range(2):
                T.op("dve", lambda e, dh=dh: e.tensor_tensor(out=qm3[:, dh], in0=stq3[:, hl * 2 + dh, :].unsqueeze(1).to_broadcast([128, 16, 16]),
                                                             in1=diagbc[:, :].rearrange("p (s j) -> p s j", s=16), op=ALU.mult), r=["stq", "diagbc"], w=["qm_s"])
            T.op("dve", lambda e: e.tensor_scalar(out=Hs[:, 0:256], in0=stv[0:16, hl * 256:(hl + 1) * 256], scalar1=stab[:, 40 + h:41 + h], scalar2=None, op0=ALU.mult),
                 r=["stv", "stab"], w=["Hs"])
            T.op("dve", lambda e: e.tensor_copy(out=Hs[:, 256:257], in_=stab[:, 40 + h:41 + h]), r=["stab", "Hs"], w=["Hs"])
            its = [(s_, dh) for s_ in range(16) for dh in range(2)]
            base = slot_rot[0]
            slot_rot[0] += len(its)

            def load(j):
                s_, dh = its[j]
                sl = (base + j) % NSL
                T.dma("sp", f"ci{sl}", cin[sl][:, :], caug_in_d[s_, h, dh * 128:(dh + 1) * 128, :], w=[f"cin{sl}"])
            for j in range(min(PFD, len(its))):
                load(j)
            i_q, ps_q = PS(hold=True)
            for j, (s_, dh) in enumerate(its):
                sl = (base + j) % NSL
                if j + PFD < len(its):
                    load(j + PFD)
                vm = vm_s[:, (s_ % 4) * 257:(s_ % 4 + 1) * 257]; vk = f"vm{s_ % 4}"
                if dh == 0:
                    T.op("pool", lambda e: e.tensor_scalar(out=vm, in0=Hs[:, 0:257], scalar1=diag16[:, s_:s_ + 1], scalar2=0.0, op0=ALU.mult, op1=ALU.add), r=["Hs", "diag16"], w=[vk])
                T.op("act", lambda e: e.activation(out=cinb[sl][:, :], in_=cin[sl][:, :], func=AF.Copy), r=[f"cin{sl}"], w=[f"cinb{sl}"])
                T.op("pe", lambda e: e.matmul(ps_q[0:16, 0:257], qm3[:, dh, s_, :], cinb[sl][:, :], start=(j == 0), stop=(j == len(its) - 1)),
                     r=["qm_s", f"cinb{sl}"], w=[("ps", i_q)])
                i_u, ps_u = PS()
                T.op("pe", lambda e: e.matmul(ps_u[:, 0:257], qk_s[0:16, 256 + dh * 128:256 + (dh + 1) * 128], vm, start=True, stop=True), r=["qk_s", vk], w=[("ps", i_u)])
                T.op("dve", lambda e: e.scalar_tensor_tensor(out=cout[sl][:, :], in0=cin[sl][:, :], scalar=decbc[:, s_ * 4 + h:s_ * 4 + h + 1], in1=ps_u[:, 0:257], op0=ALU.mult, op1=ALU.add),
                     r=[f"cin{sl}", "decbc", ("ps", i_u)], w=[f"cout{sl}"])
                T.dma("sp", f"co{sl}", scaug_d[s_, h, dh * 128:(dh + 1) * 128, :], cout[sl][:, :], r=[f"cout{sl}"])
                yield
            ps_held.discard(i_q)
            T.op("dve", lambda e: e.tensor_scalar(out=Hs[:, 0:257], in0=ps_q[0:16, 0:257], scalar1=stab[:, 24 + h:25 + h], scalar2=None, op0=ALU.mult),
                 r=[("ps", i_q), "stab"] + [f"vm{j}" for j in range(4)], w=["Hs"])
            T.op("dve", lambda e: e.scalar_tensor_tensor(out=Hs[:, 0:256], in0=stv[0:16, hl * 256:(hl + 1) * 256], scalar=stab[:, 36 + h:37 + h], in1=Hs[:, 0:256], op0=ALU.mult, op1=ALU.add),
                 r=["stv", "stab_q", "Hs"], w=["Hs"])
            T.op("dve", lambda e: e.tensor_tensor(out=Hs[:, 256:257], in0=Hs[:, 256:257], in1=stab[:, 36 + h:37 + h], op=ALU.add), r=["stab_q", "Hs"], w=["Hs"])
            post_mlstm(Hs[:, 0:257], ["Hs"], stab[:, 28 + h:29 + h], ["stab"], 16, 2, lambda k, h=h: to_Y(k, 16, h * 2, NP))
            yield

    def gla_samples(hp):
        for hl, h in enumerate((2 * hp, 2 * hp + 1)):
            i, ps = PS()
            psb = ps[:, :].bitcast(BF16)
            T.op("pe", lambda e: e.transpose(psb[0:16, 0:128], stq3[:, hl, :], ident_bf[:]), r=["stq", "ident_bf"], w=[("ps", i)])
            T.op("pe", lambda e: e.transpose(psb[0:16, 128:256], stk3[:, hl, :], ident_bf[:]), r=["stk", "ident_bf"], w=[("ps", i)])
            T.op("pe", lambda e: e.transpose(psb[0:16, 256:384], stk3[:, 2 + hl, :], ident_bf[:]), r=["stk", "ident_bf"], w=[("ps", i)])
            T.op("act", lambda e: e.activation(out=qk_s[:, 0:384], in_=psb[0:16, 0:384], func=AF.Copy), r=[("ps", i)], w=["qk_s"])
            T.op("dve", lambda e: e.scalar_tensor_tensor(out=junk16[:, 0:128], in0=qk_s[:, 0:128], scalar=1.0, in1=qk_s[:, 128:256], op0=ALU.mult, op1=ALU.mult,
                                                         accum_out=stab[:, 44 + h:45 + h]), r=["qk_s", "junk16"], w=["junk16", "stab_q"])
            qm3 = qm_s[:, 0:256].rearrange("p (s j) -> p s j", s=16)
            T.op("dve", lambda e: e.tensor_tensor(out=qm3, in0=stq3[:, hl, :].unsqueeze(1).to_broadcast([128, 16, 16]),
                                                  in1=diagbc[:, :].rearrange("p (s j) -> p s j", s=16), op=ALU.mult), r=["stq", "diagbc"], w=["qm_s"])
            base = slot_rot[0]
            slot_rot[0] += 16

            def load(j):
                sl = (base + j) % NSL
                T.dma("sp", f"ci{sl}", cin[sl][:, 0:256], s_in_d[j, h, :, :], w=[f"cin{sl}"])
            for j in range(PFD):
                load(j)
            i_q, ps_q = PS(hold=True)
            for s_ in range(16):
                sl = (base + s_) % NSL
                if s_ + PFD < 16:
                    load(s_ + PFD)
                vm = vm_s[:, (s_ % 4) * 257:(s_ % 4) * 257 + 256]; vk = f"vm{s_ % 4}"
                T.op("pool", lambda e: e.tensor_scalar(out=vm, in0=stv[0:16, hl * 256:(hl + 1) * 256], scalar1=diag16[:, s_:s_ + 1], scalar2=0.0, op0=ALU.mult, op1=ALU.add), r=["stv", "diag16"], w=[vk])
                T.op("act", lambda e: e.activation(out=cinb[sl][:, 0:256], in_=cin[sl][:, 0:256], func=AF.Copy), r=[f"cin{sl}"], w=[f"cinb{sl}"])
                T.op("pe", lambda e: e.matmul(ps_q[0:16, 0:256], qm3[:, s_, :], cinb[sl][:, 0:256], start=(s_ == 0), stop=(s_ == 15)), r=["qm_s", f"cinb{sl}"], w=[("ps", i_q)])
                i_u, ps_u = PS()
                T.op("pe", lambda e: e.matmul(ps_u[:, 0:256], qk_s[0:16, 256:384], vm, start=True, stop=True), r=["qk_s", vk], w=[("ps", i_u)])
                T.op("dve", lambda e: e.scalar_tensor_tensor(out=cout[sl][:, 0:256], in0=cin[sl][:, 0:256], scalar=elast_s[:, h * 16 + s_:h * 16 + s_ + 1], in1=ps_u[:, 0:256], op0=ALU.mult, op1=ALU.add),
                     r=[f"cin{sl}", "elast_s", ("ps", i_u)], w=[f"cout{sl}"])
                T.dma("sp", f"co{sl}", ss_d[s_, h, :, :], cout[sl][:, 0:256], r=[f"cout{sl}"])
                yield
            ps_held.discard(i_q)
            T.op("dve", lambda e: e.scalar_tensor_tensor(out=Hs[:, 0:256], in0=stv[0:16, hl * 256:(hl + 1) * 256], scalar=stab[:, 44 + h:45 + h], in1=ps_q[0:16, 0:256], op0=ALU.mult, op1=ALU.add),
                 r=["stv", "stab_q", ("ps", i_q)], w=["Hs"])
            post_gla(Hs[:, 0:256], ["Hs"], 16, 2, lambda k, h=h: to_Y(k, 16, 8 + h * 2, NP))
            yield

    wprefetch(C_K + 0 * 512)
    tok3 = mlstm_gates(xTp3, XPK, False, X)
    GK = ["g_itil", "g_logf", "g_b", "g_g", "g_st", "g_big", "tokT"]
    for q4 in range(4):
        T.dma("pool", f"xm{q4}", X3[:, 4 * q4:4 * q4 + 4, :], xT_r[:, 4 * q4:4 * q4 + 4, :], r=["tokT"], w=(["X"] if q4 == 3 else [f"X_{q4}"]) + (GK[:-1] if q4 == 0 else []))
    mlstm_segment(0, False, tok3, nxt=lambda: wprefetch(C_K + 512))
    mlstm_segment(1, False, tok3, nxt=lambda: wprefetch(C_BK, 256))
    T.barrier()
    gla_bg(xTp3, XPK, TB_PRE)
    gla_segment(0, False, nxt=lambda: wprefetch(C_BK + 256, 256))
    gla_segment(1, False)
    T.barrier()
    CK = [f"Caug{h}" for h in range(4)]; SK = [f"Sst{h}" for h in range(4)]
    T.op("dve", lambda e: e.tensor_scalar(out=Caug[:], in0=Caug[:], scalar1=flag[:, 0:1], scalar2=None, op0=ALU.mult), r=["flag"] + CK, w=CK)
    T.op("act", lambda e: e.activation(out=Cbf[:], in_=Caug[:], func=AF.Copy), r=CK, w=[f"Cbf{h}" for h in range(4)])
    T.op("dve", lambda e: e.tensor_scalar(out=Sst[:], in0=Sst[:], scalar1=flag[:, 0:1], scalar2=None, op0=ALU.mult), r=["flag"] + SK, w=SK)
    T.op("act", lambda e: e.activation(out=Sbf[:], in_=Sst[:], func=AF.Copy), r=SK, w=[f"Sbf{h}" for h in range(4)])
    T.op("dve", lambda e: e.tensor_scalar(out=mstate[:], in0=mstate[:], scalar1=flag[0:4, 0:1], scalar2=None, op0=ALU.mult), r=["flag", "mstate"], w=["mstate"])
    T.barrier()
    wviews[:] = WV2
    wslot[0] = 0
    wprefetch(C_Q)
    tok3 = mlstm_gates(X3, XK, True, Y)
    sample_gates()
    DEFER = False
    if not DEFER:
        def seg_m(hp, pref):
            def f():
                stash_mlstm()
                pref()
            return f
        bg = make_bg(mlstm_samples(0))
        mlstm_segment(0, True, tok3, bg=bg, nxt=seg_m(0, lambda: wprefetch(C_Q + 512)))
        bg.drain()
        bg = make_bg(mlstm_samples(1))
        mlstm_segment(1, True, tok3, bg=bg, nxt=seg_m(1, lambda: wprefetch(C_BQ, 256)))
        bg.drain()
        T.barrier()
        gla_bg(X3, XK, TB_MAIN)
        bg = make_bg(gla_samples(0))
        gla_segment(0, True, bg=bg, nxt=lambda: (stash_gla(), wprefetch(C_BQ + 256, 256)))
        bg.drain()
        bg = make_bg(gla_samples(1))
        gla_segment(1, True, bg=bg, nxt=lambda: (stash_gla(), wprefetch(C_O), wprefetch(C_Z)))
        bg.drain()
        bgB1 = make_bg(iter(()))
        T.barrier()
    else:
        def nx(pref, stash):
            def f():
                stash()
                pref()
            return f
        mlstm_segment(0, True, tok3, nxt=nx(lambda: wprefetch(C_Q + 512), stash_mlstm))
        bgA0 = make_bg(mlstm_samples(0))
        mlstm_segment(1, True, tok3, bg2=bgA0, bgn=2, nxt=lambda: (bgA0.drain(), stash_mlstm(), wprefetch(C_BQ, 256)))
        T.barrier()
        gla_bg(X3, XK, TB_MAIN)
        bgA1 = make_bg(mlstm_samples(1))
        gla_segment(0, True, bg2=bgA1, bgn=4, nxt=lambda: (bgA1.drain(), stash_gla(), wprefetch(C_BQ + 256, 256)))
        bgB0 = make_bg(gla_samples(0))
        gla_segment(1, True, bg2=bgB0, bgn=2, nxt=lambda: (bgB0.drain(), stash_gla()))
        bgB1 = make_bg(gla_samples(1))
        T.barrier()
    wviews[:] = WV4
    wslot[0] = 2
    for h in range(4):
        for dh in range(2):
            sl = slice((h * 2 + dh) * 257, (h * 2 + dh + 1) * 257)
            T.dma("sp", f"o_pc{h}{dh}", pcaug_d[h, dh * 128:(dh + 1) * 128, :], Caug[:, sl], r=[f"Caug{h}"])
        T.dma("sp", f"o_ps{h}", ps_d[h, :, :], Sst[:, h * 256:(h + 1) * 256], r=[f"Sst{h}"])
    T.dma("sp", "o_pm", pm_d[:, :], mstate[:, :], r=["mstate"])
    T.dma("sp", "o_pconv", pconv_d[:, :], pconv_sb[:, :], r=["pconv_sb"])
    T.dma("sp", "o_sconv", sconv_d[:, :], sconv_sb[:, :], r=["sconv_sb"])

    for g in range(2):
        Wo, Wok = wload(C_O + g * 512)
        Wz, Wzk = wload(C_Z + g * 512)
        for cbi in range(4):
            blk = g * 4 + cbi
            for (t0, n) in TB_MAIN:
                io, pso = fm_proj(Wo, Wok, cbi, X3, XK, t0, n)
                iz, psz = fm_proj(Wz, Wzk, cbi, X3, XK, t0, n)
                T.op("act", lambda e: e.activation(out=accb[0][:, 0:n], in_=pso[:, 0:n], func=AF.Sigmoid), r=[("ps", io)], w=["acc0"])
                T.op("act", lambda e: e.activation(out=accb[1][:, 0:n], in_=psz[:, 0:n], func=AF.Sigmoid), r=[("ps", iz)], w=["acc1"])
                T.op("dve", lambda e: e.tensor_tensor(out=accb[0][:, 0:n], in0=accb[0][:, 0:n], in1=accb[1][:, 0:n], op=ALU.mult), r=["acc0", "acc1"], w=["acc0"])
                T.op("dve", lambda e: e.tensor_tensor(out=accb[0][:, 0:n], in0=accb[0][:, 0:n], in1=psz[:, 0:n], op=ALU.mult), r=["acc0", ("ps", iz)], w=["acc0"])
                T.op("dve", lambda e: e.scalar_tensor_tensor(out=Y3[:, blk, t0:t0 + n], in0=Y3[:, blk, t0:t0 + n], scalar=gna[:, blk:blk + 1], in1=accb[0][:, 0:n], op0=ALU.mult, op1=ALU.mult),
                     r=["Y", "gna", "acc0"], w=["Y"])
                bgB1(2)
    bgB1.drain()
    for g in range(2):
        Wz, Wzk = wload(C_BZ + g * 512)
        for cbi in range(4):
            blk = g * 4 + cbi
            for (t0, n) in TB_MAIN:
                iz, psz = fm_proj(Wz, Wzk, cbi, X3, XK, t0, n)
                T.op("act", lambda e: e.activation(out=accb[1][:, 0:n], in_=psz[:, 0:n], func=AF.Sigmoid), r=[("ps", iz)], w=["acc1"])
                T.op("dve", lambda e: e.tensor_tensor(out=accb[1][:, 0:n], in0=accb[1][:, 0:n], in1=psz[:, 0:n], op=ALU.mult), r=["acc1", ("ps", iz)], w=["acc1"])
                T.op("dve", lambda e: e.scalar_tensor_tensor(out=Y3[:, 8 + blk, t0:t0 + n], in0=Y3[:, 8 + blk, t0:t0 + n], scalar=gnb[:, blk:blk + 1], in1=accb[1][:, 0:n], op0=ALU.mult, op1=ALU.mult),
                     r=["Y", "gnb", "acc1"], w=["Y"])
    T.barrier()

    M3 = QR[:, 0:16 * NM].rearrange("p (a b) -> p a b", a=16)
    w_pa_r = w_pa_d.rearrange("(kc p) n -> p kc n", p=128)
    w_pb_r = w_pb_d.rearrange("(kc p) n -> p kc n", p=128)
    for fbg in range(8):
        s2 = fbg % 2
        base = s2 * 12288
        Wpa3 = A[:, base:base + 2048].rearrange("p (a b) -> p a b", a=8)
        Wpb3 = A[:, base + 2048:base + 4096].rearrange("p (a b) -> p a b", a=8)
        Wga3 = A[:, base + 4096:base + 8192].rearrange("p (a b) -> p a b", a=16)
        Wgb3 = A[:, base + 8192:base + 12288].rearrange("p (a b) -> p a b", a=16)
        c0 = fbg * 256
        T.dma("pool", f"v{s2}a", Wpa3, w_pa_r[:, :, c0:c0 + 256], w=[f"V{s2}a"])
        T.dma("pool", f"v{s2}b", Wpb3, w_pb_r[:, :, c0:c0 + 256], w=[f"V{s2}b"])
        T.dma("pool", f"v{s2}c", Wga3, w_in_r[:, :, C_GA + c0:C_GA + c0 + 256], w=[f"V{s2}c"])
        T.dma("pool", f"v{s2}d", Wgb3, w_in_r[:, :, C_GB + c0:C_GB + c0 + 256], w=[f"V{s2}d"])
        for fbl in range(2):
            fb = fbg * 2 + fbl
            for (t0, n) in TB_MAIN:
                ia, psa = PS()
                for kc in range(8):
                    T.op("pe", lambda e, kc=kc: e.matmul(psa[:, 0:n], Wpa3[:, kc, fbl * 128:(fbl + 1) * 128], Y3[:, kc, t0:t0 + n], start=(kc == 0), stop=(kc == 7)),
                         r=[f"V{s2}a", "Y"], w=[("ps", ia)])
                ib, psb_ = PS()
                for kc in range(8):
                    T.op("pe", lambda e, kc=kc: e.matmul(psb_[:, 0:n], Wpb3[:, kc, fbl * 128:(fbl + 1) * 128], Y3[:, 8 + kc, t0:t0 + n], start=(kc == 0), stop=(kc == 7)),
                         r=[f"V{s2}b", "Y"], w=[("ps", ib)])
                iga, psga = fm_proj(Wga3, [f"V{s2}c"], fbl, X3, XK, t0, n)
                igb, psgb = fm_proj(Wgb3, [f"V{s2}d"], fbl, X3, XK, t0, n)
                T.op("act", lambda e: e.activation(out=accb[0][:, 0:n], in_=psga[:, 0:n], func=AF.Sigmoid), r=[("ps", iga)], w=["acc0"])
                T.op("act", lambda e: e.activation(out=accb[1][:, 0:n], in_=psgb[:, 0:n], func=AF.Sigmoid), r=[("ps", igb)], w=["acc1"])
                T.op("dve", lambda e: e.tensor_tensor(out=accb[0][:, 0:n], in0=accb[0][:, 0:n], in1=psa[:, 0:n], op=ALU.mult), r=["acc0", ("ps", ia)], w=["acc0"])
                T.op("dve", lambda e: e.tensor_tensor(out=accb[1][:, 0:n], in0=accb[1][:, 0:n], in1=psb_[:, 0:n], op=ALU.mult), r=["acc1", ("ps", ib)], w=["acc1"])
                T.op("dve", lambda e: e.tensor_tensor(out=M3[:, fb, t0:t0 + n], in0=accb[0][:, 0:n], in1=accb[1][:, 0:n], op=ALU.add), r=["acc0", "acc1"], w=["M"])
    T.barrier()

    wout3 = XA[:, 0:16 * 2048].rearrange("p (a b) -> p a b", a=16)
    w_out_r = w_out_d.rearrange("(kc p) n -> p kc n", p=128)
    for q4 in range(4):
        T.dma("pool", f"wo{q4}", wout3[:, q4 * 4:(q4 + 1) * 4, :], w_out_r[:, q4 * 4:(q4 + 1) * 4, :], w=[f"wout{q4}"])
    WOK = [f"wout{q4}" for q4 in range(4)]
    lnbase = 16 * 2048
    lng = XA[:, lnbase:lnbase + 4096].bitcast(F32)
    lnb = XA[:, lnbase + 4096:lnbase + 8192].bitcast(F32)
    T.dma("sp", "c_lng", lng, lng_d[:, :], w=["lng"])
    T.dma("sp", "c_lnb", lnb, lnb_d[:, :], w=["lnb"])
    zt = [Y[:, k * 4096:(k + 1) * 4096].bitcast(F32) for k in range(2)]
    xt_ = [Y[:, 8192 + k * 4096:8192 + (k + 1) * 4096].bitcast(F32) for k in range(2)]
    def xt_load(tl):
        m = 128 if tl < 8 else 16
        k = tl % 2
        T.dma("sp", f"xt{k}", xt_[k][0:m, :], xtok_d[tl * 128:tl * 128 + m, :], w=[f"xt{k}"])
    xt_load(0)
    for tl in range(9):
        m = 128 if tl < 8 else 16
        k = tl % 2
        if tl + 1 < 9:
            xt_load(tl + 1)
        sm = small[k]; sk = f"small{k}"
        for g in range(4):
            i, ps = PS()
            for kc in range(16):
                T.op("pe", lambda e, kc=kc: e.matmul(ps[0:m, 0:512], M3[:, kc, tl * 128:tl * 128 + m], wout3[:, kc, g * 512:(g + 1) * 512], start=(kc == 0), stop=(kc == 15)),
                     r=["M", f"wout{kc // 4}"], w=[("ps", i)])
            T.op("dve", lambda e: e.scalar_tensor_tensor(out=zt[k][0:m, g * 512:(g + 1) * 512], in0=xt_[k][0:m, g * 512:(g + 1) * 512], scalar=float(ALPHA), in1=ps[0:m, 0:512], op0=ALU.mult, op1=ALU.add),
                 r=[f"xt{k}", ("ps", i)], w=[f"zt{k}"])
            T.op("dve", lambda e: e.bn_stats(out=sm[0:m, 8 + g * 6:14 + g * 6], in_=zt[k][0:m, g * 512:(g + 1) * 512]), r=[f"zt{k}", sk], w=[sk])
        T.op("dve", lambda e: e.bn_aggr(out=sm[0:m, 2:4], in_=sm[0:m, 8:32]), r=[sk], w=[sk])
        T.op("dve", lambda e: e.tensor_scalar(out=sm[0:m, 1:2], in0=sm[0:m, 3:4], scalar1=EPS, scalar2=None, op0=ALU.add), r=[sk], w=[sk])
        T.op("act", lambda e: e.activation(out=sm[0:m, 4:5], in_=sm[0:m, 1:2], func=AF.Ln), r=[sk], w=[sk])
        T.op("act", lambda e: e.activation(out=sm[0:m, 5:6], in_=sm[0:m, 4:5], func=AF.Exp, scale=-0.5), r=[sk], w=[sk])
        T.op("dve", lambda e: e.tensor_scalar(out=zt[k][0:m, :], in0=zt[k][0:m, :], scalar1=sm[0:m, 2:3], scalar2=sm[0:m, 5:6], op0=ALU.subtract, op1=ALU.mult), r=[f"zt{k}", sk], w=[f"zt{k}"])
        T.op("pool", lambda e: e.tensor_tensor(out=zt[k][0:m, :], in0=zt[k][0:m, :], in1=lng[0:m, :], op=ALU.mult), r=[f"zt{k}", "lng"], w=[f"zt{k}"])
        T.op("pool", lambda e: e.tensor_tensor(out=zt[k][0:m, :], in0=zt[k][0:m, :], in1=lnb[0:m, :], op=ALU.add), r=[f"zt{k}", "lnb"], w=[f"zt{k}"])
        T.dma("sp", f"oy{k}", y_d[tl * 128:tl * 128 + m, :], zt[k][0:m, :], r=[f"zt{k}"])
    T.finish()
    return nc


_CACHE = {}


def _host_inputs(inp):
    f = np.float32
    x_prompt = np.asarray(inp["x_prompt"], f); x_sample = np.asarray(inp["x_sample"], f)
    C = np.asarray(inp["state_mlstm_C"], f)[0]; n = np.asarray(inp["state_mlstm_n"], f)[0]
    m = np.asarray(inp["state_mlstm_m"], f)[0]; cv = np.asarray(inp["state_conv"], f)[0]
    S = np.asarray(inp["state_gla_S"], f)[0]
    caug = np.concatenate([C, n[..., None]], axis=-1)
    w_in = np.ascontiguousarray(np.asarray(inp["w_in"], f)[0])
    w_pa = np.ascontiguousarray(np.asarray(inp["w_pa"], f)[0]); w_pb = np.ascontiguousarray(np.asarray(inp["w_pb"], f)[0])
    w_out = np.ascontiguousarray(np.asarray(inp["w_out"], f)[0]); w_gu = np.ascontiguousarray(np.asarray(inp["w_gate_up"], f)[0])
    conv_w = np.asarray(inp["conv_w"], f)[0]; conv_b = np.asarray(inp["conv_b"], f)[0]
    cw = np.ascontiguousarray(conv_w.T.reshape(16, 128, 4).transpose(1, 0, 2).reshape(128, 64))
    cb = np.ascontiguousarray(conv_b.reshape(16, 128).T)
    bgate = np.ascontiguousarray(np.asarray(inp["b_gate"], f)[0].reshape(4, 128).T)
    b_i = np.asarray(inp["b_i"], f)[0]; b_f = np.asarray(inp["b_f"], f)[0]
    gna = np.ascontiguousarray(np.asarray(inp["a_norm_g"], f)[0].reshape(8, 128).T)
    gnb = np.ascontiguousarray(np.asarray(inp["b_norm_g"], f)[0].reshape(8, 128).T)
    lng = np.ascontiguousarray(np.broadcast_to(np.asarray(inp["ln_g"], f)[0][None, :], (128, D)))
    lnb = np.ascontiguousarray(np.broadcast_to(np.asarray(inp["ln_b"], f)[0][None, :], (128, D)))
    ident = np.eye(128, dtype=f)
    tri = (np.arange(128)[:, None] <= np.arange(128)[None, :]).astype(f)
    rmask = np.ones((128, NM), f); rmask[:, 0:NP:128] = 0.0; rmask[:, NP:] = 0.0
    shared = dict(w_in=w_in, w_pa=w_pa, w_pb=w_pb, w_out=w_out, w_gu=w_gu, cw=cw, cb=cb, bgate=bgate,
                  bi=b_i.reshape(4, 1).copy(), bf=b_f.reshape(4, 1).copy(),
                  bi_row=np.ascontiguousarray(np.broadcast_to(b_i[None, :], (16, 4))),
                  bf_row=np.ascontiguousarray(np.broadcast_to(b_f[None, :], (16, 4))),
                  gna=gna, gnb=gnb, lng=lng, lnb=lnb, ident=ident, maskA=tri * f(1.0 / 16.0), maskB=tri.copy(),
                  rmask=rmask, diag16=np.eye(16, dtype=f),
                  diagbc=np.ascontiguousarray(np.broadcast_to(np.eye(16, dtype=f).reshape(1, 256), (128, 256))))
    maps = []
    for c in range(8):
        b, th = c // 2, c % 2
        xm = np.concatenate([x_prompt[b, th * NP:(th + 1) * NP], x_sample[c * NS:(c + 1) * NS, 0]], axis=0)
        d = dict(shared)
        d["xT"] = np.ascontiguousarray(xm.T); d["xtok"] = np.ascontiguousarray(xm)
        d["xTp"] = np.ascontiguousarray(x_prompt[b, 0:NP].T)
        d["flag"] = np.full((128, 1), float(th), f)
        d["caug_in"] = np.ascontiguousarray(caug[c * NS:(c + 1) * NS]); d["s_in"] = np.ascontiguousarray(S[c * NS:(c + 1) * NS])
        d["m_in"] = np.ascontiguousarray(m[c * NS:(c + 1) * NS])
        cvs = cv[c * NS:(c + 1) * NS]
        d["conv_in"] = np.ascontiguousarray(cvs.transpose(2, 1, 0).reshape(16, 128, 3, 16).transpose(1, 0, 2, 3).reshape(128, 16 * 3 * 16))
        maps.append(d)
    return maps


def kernel(**inputs):
    if "nc" not in _CACHE:
        _CACHE["nc"] = build_nc()
    nc = _CACHE["nc"]
    maps = _host_inputs(inputs)
    res = run_bass_kernel_spmd(nc, maps, core_ids=list(range(8)))
    R = res.results
    f = np.float32
    y_prompt = np.zeros((4, 2048, D), f); y_sample = np.zeros((128, 1, D), f)
    pC = np.zeros((1, 4, 4, 256, 256), f); pn = np.zeros((1, 4, 4, 256), f); pm = np.zeros((1, 4, 4), f)
    pconv = np.zeros((1, 4, 3, 2048), f); pS = np.zeros((1, 4, 4, 128, 256), f)
    sC = np.zeros((1, 128, 4, 256, 256), f); sn = np.zeros((1, 128, 4, 256), f); sm = np.zeros((1, 128, 4), f)
    sconv = np.zeros((1, 128, 3, 2048), f); sS = np.zeros((1, 128, 4, 128, 256), f)
    for c in range(8):
        b, th = c // 2, c % 2
        r = R[c]
        y = np.asarray(r["y"])
        y_prompt[b, th * NP:(th + 1) * NP] = y[0:NP]
        y_sample[c * NS:(c + 1) * NS, 0] = y[NP:NM]
        if th == 1:
            ca = np.asarray(r["pcaug"])
            pC[0, b] = ca[:, :, 0:256]; pn[0, b] = ca[:, :, 256]
            pm[0, b] = np.asarray(r["pm"])[:, 0]
            pS[0, b] = np.asarray(r["ps"])
            pconv[0, b] = np.asarray(r["pconv"]).reshape(128, 16, 3).transpose(2, 1, 0).reshape(3, 2048)
        sa = np.asarray(r["scaug"])
        sC[0, c * NS:(c + 1) * NS] = sa[..., 0:256]; sn[0, c * NS:(c + 1) * NS] = sa[..., 256]
        sm[0, c * NS:(c + 1) * NS] = np.asarray(r["sm"])
        sS[0, c * NS:(c + 1) * NS] = np.asarray(r["ss"])
        sconv[0, c * NS:(c + 1) * NS] = np.asarray(r["sconv"]).reshape(128, 16, 3, 16).transpose(3, 2, 1, 0).reshape(16, 3, 2048)
    return (y_prompt, y_sample, pC, pn, pm, pconv, pS, sC, sn, sm, sconv, sS)
```
